# Optimizing a Trainium2 kernel written in Bass

```python
import jax, jax.numpy as jnp
from jax import lax
import numpy as np

D_MODEL = 2048
BATCH = 2
SEQ = 4096
DEPTH = 2
DEC_BATCH = 8
DEC_SEQ = 16
PAST_LEN = 1024

CHUNK = 64
N_PREV_CHUNKS = 8
BAND_PAST = N_PREV_CHUNKS * CHUNK
BAND = (N_PREV_CHUNKS + 1) * CHUNK
POOL_WIDTH = D_MODEL // 2
POOL_WINDOWS = (2, 4, 8, 16)
N_POOL_GROUPS = len(POOL_WINDOWS)
POOL_GROUP = POOL_WIDTH // N_POOL_GROUPS
POOL_HIST = max(POOL_WINDOWS) - 1
HEAD_DIM = 128
ATTN_WIDTH = D_MODEL // 2
N_HEADS = ATTN_WIDTH // HEAD_DIM
MAX_REL = 128
D_FF = 4 * D_MODEL
EPS = 1e-6
ATTN_SCALE = HEAD_DIM ** -0.5
NEG_INF = -1e30
IN_SPLITS = (POOL_WIDTH, POOL_WIDTH + ATTN_WIDTH, POOL_WIDTH + 2 * ATTN_WIDTH, POOL_WIDTH + 3 * ATTN_WIDTH)
IN_WIDTH = POOL_WIDTH + 3 * ATTN_WIDTH + 2 * D_MODEL

kernel_name = "chunk_streaming_pool_band_attn_hybrid"


def rmsnorm(x, g):
    xf = x.astype(jnp.float32)
    y = xf * lax.rsqrt(jnp.mean(xf * xf, axis=-1, keepdims=True) + EPS)
    return (y * g.astype(jnp.float32)).astype(x.dtype)


def mixer_inputs(x, norm1_g, w_in, b_gate, q_norm_g, k_norm_g):
    n, t, _ = x.shape
    h = rmsnorm(x, norm1_g)
    z = h @ w_in
    p, q, k, v, g = jnp.split(z, IN_SPLITS, axis=-1)
    q = rmsnorm(q.reshape(n, t, N_HEADS, HEAD_DIM), q_norm_g)
    k = rmsnorm(k.reshape(n, t, N_HEADS, HEAD_DIM), k_norm_g)
    v = v.reshape(n, t, N_HEADS, HEAD_DIM)
    gates = jax.nn.sigmoid((g + b_gate).astype(jnp.float32)).astype(x.dtype)
    return p, q, k, v, gates


def pool_mixer(p, hist, pos0, pool_w, pool_scale):
    n, t, _ = p.shape
    xp = jnp.concatenate([hist.astype(p.dtype), p], axis=1).astype(jnp.float32)
    cs = jnp.pad(jnp.cumsum(xp, axis=1), ((0, 0), (1, 0), (0, 0)))
    pos = pos0 + jnp.arange(t)
    pf = p.astype(jnp.float32)
    outs = []
    for gi, w in enumerate(POOL_WINDOWS):
        sl = slice(gi * POOL_GROUP, (gi + 1) * POOL_GROUP)
        end = cs[:, POOL_HIST + 1:POOL_HIST + 1 + t, sl]
        start = cs[:, POOL_HIST + 1 - w:POOL_HIST + 1 - w + t, sl]
        cnt = jnp.minimum(pos + 1, w).astype(jnp.float32)[None, :, None]
        d = ((end - start) / cnt - pf[:, :, sl]).astype(p.dtype)
        outs.append(jnp.einsum('ntc,cd->ntd', d, pool_w[gi]))
    return jnp.concatenate(outs, axis=-1) * pool_scale


def band_attention_prompt(q, k, v, rel_bias):
    n, s, h, d = q.shape
    nc = s // CHUNK
    pad = ((0, 0), (BAND_PAST, 0), (0, 0), (0, 0))
    kp = jnp.pad(k, pad).reshape(n, nc + N_PREV_CHUNKS, CHUNK, h, d)
    vp = jnp.pad(v, pad).reshape(n, nc + N_PREV_CHUNKS, CHUNK, h, d)
    kb = jnp.concatenate([kp[:, j:j + nc] for j in range(N_PREV_CHUNKS + 1)], axis=2)
    vb = jnp.concatenate([vp[:, j:j + nc] for j in range(N_PREV_CHUNKS + 1)], axis=2)
    qc = q.reshape(n, nc, CHUNK, h, d)
    scores = jnp.einsum('ncqhd,nckhd->nchqk', qc, kb,
                        preferred_element_type=jnp.float32) * ATTN_SCALE
    qi = jnp.arange(CHUNK)
    kj = jnp.arange(BAND)
    rel = qi[:, None] - (kj[None, :] - BAND_PAST)
    bias = rel_bias[:, jnp.clip(rel, -MAX_REL, MAX_REL) + MAX_REL].astype(jnp.float32)
    kpos = jnp.arange(nc)[:, None] * CHUNK - BAND_PAST + kj[None, :]
    scores = jnp.where((kpos >= 0)[None, :, None, None, :], scores + bias[None, None], NEG_INF)
    probs = jax.nn.softmax(scores, axis=-1).astype(v.dtype)
    out = jnp.einsum('nchqk,nckhd->ncqhd', probs, vb)
    return out.reshape(n, s, h * d)


def band_attention_sample(q, k, v, ck, cv, rel_bias):
    n, t, h, d = q.shape
    L = ck.shape[1]
    kk = jnp.concatenate([ck.astype(k.dtype), k], axis=1)
    vv = jnp.concatenate([cv.astype(v.dtype), v], axis=1)
    scores = jnp.einsum('nqhd,nkhd->nhqk', q, kk,
                        preferred_element_type=jnp.float32) * ATTN_SCALE
    qpos = PAST_LEN + jnp.arange(t)
    kpos = PAST_LEN - L + jnp.arange(L + t)
    rel = qpos[:, None] - kpos[None, :]
    bias = rel_bias[:, jnp.clip(rel, -MAX_REL, MAX_REL) + MAX_REL].astype(jnp.float32)
    probs = jax.nn.softmax(scores + bias[None], axis=-1).astype(v.dtype)
    out = jnp.einsum('nhqk,nkhd->nqhd', probs, vv).reshape(n, t, h * d)
    return out, kk[:, t:], vv[:, t:]


def merge_and_ffn(x, pool_out, attn_out, gates, w_branch_a, w_branch_b, w_out,
                  norm2_g, w_up, w_down):
    ga, gb = jnp.split(gates, 2, axis=-1)
    merged = ga * (pool_out @ w_branch_a) + gb * (attn_out @ w_branch_b)
    x = x + merged @ w_out
    u = jnp.square(jax.nn.relu(rmsnorm(x, norm2_g) @ w_up))
    return x + u @ w_down


def setup_inputs(seed: int = 0) -> dict:
    key = jax.random.key(seed)
    ks = jax.random.split(key, 20)
    f32 = jnp.float32
    L = min(BAND_PAST, PAST_LEN)
    nrm = lambda k, shape, s: jax.random.normal(k, shape, f32) * s
    return {
        "x_prompt": nrm(ks[0], (BATCH, SEQ, D_MODEL), 1.0),
        "x_sample": nrm(ks[1], (DEC_BATCH, DEC_SEQ, D_MODEL), 1.0),
        "state_pool": nrm(ks[2], (DEPTH, DEC_BATCH, POOL_HIST, POOL_WIDTH), 1.0),
        "cache_k": nrm(ks[3], (DEPTH, DEC_BATCH, L, N_HEADS, HEAD_DIM), 1.0),
        "cache_v": nrm(ks[4], (DEPTH, DEC_BATCH, L, N_HEADS, HEAD_DIM), 1.0),
        "norm1_g": 1.0 + nrm(ks[5], (DEPTH, D_MODEL), 0.05),
        "w_in": nrm(ks[6], (DEPTH, D_MODEL, IN_WIDTH), D_MODEL ** -0.5),
        "b_gate": nrm(ks[7], (DEPTH, 2 * D_MODEL), 0.01),
        "pool_w": nrm(ks[8], (DEPTH, N_POOL_GROUPS, POOL_GROUP, POOL_GROUP), POOL_GROUP ** -0.5),
        "pool_scale": 1.0 + nrm(ks[9], (DEPTH, POOL_WIDTH), 0.05),
        "q_norm_g": 1.0 + nrm(ks[10], (DEPTH, HEAD_DIM), 0.05),
        "k_norm_g": 1.0 + nrm(ks[11], (DEPTH, HEAD_DIM), 0.05),
        "rel_bias": nrm(ks[12], (DEPTH, N_HEADS, 2 * MAX_REL + 1), 0.1),
        "w_branch_a": nrm(ks[13], (DEPTH, POOL_WIDTH, D_MODEL), POOL_WIDTH ** -0.5),
        "w_branch_b": nrm(ks[14], (DEPTH, ATTN_WIDTH, D_MODEL), ATTN_WIDTH ** -0.5),
        "w_out": nrm(ks[15], (DEPTH, D_MODEL, D_MODEL), D_MODEL ** -0.5),
        "norm2_g": 1.0 + nrm(ks[16], (DEPTH, D_MODEL), 0.05),
        "w_up": nrm(ks[17], (DEPTH, D_MODEL, D_FF), D_MODEL ** -0.5),
        "w_down": nrm(ks[18], (DEPTH, D_FF, D_MODEL), 0.5 * D_FF ** -0.5),
    }


def reference(x_prompt, x_sample, state_pool, cache_k, cache_v, norm1_g, w_in, b_gate,
              pool_w, pool_scale, q_norm_g, k_norm_g, rel_bias, w_branch_a, w_branch_b,
              w_out, norm2_g, w_up, w_down):
    xp, xs = x_prompt, x_sample
    s = xp.shape[1]
    keep_p = max(s - BAND_PAST, 0)
    pool_p, kp_l, vp_l, pool_s, ks_l, vs_l = [], [], [], [], [], []
    for l in range(DEPTH):
        p, q, k, v, gates = mixer_inputs(xp, norm1_g[l], w_in[l], b_gate[l], q_norm_g[l], k_norm_g[l])
        hist0 = jnp.zeros((xp.shape[0], POOL_HIST, POOL_WIDTH), p.dtype)
        a_out = pool_mixer(p, hist0, 0, pool_w[l], pool_scale[l])
        b_out = band_attention_prompt(q, k, v, rel_bias[l])
        pool_p.append(p[:, -POOL_HIST:])
        kp_l.append(k[:, keep_p:])
        vp_l.append(v[:, keep_p:])
        xp = merge_and_ffn(xp, a_out, b_out, gates, w_branch_a[l], w_branch_b[l], w_out[l],
                           norm2_g[l], w_up[l], w_down[l])
        p, q, k, v, gates = mixer_inputs(xs, norm1_g[l], w_in[l], b_gate[l], q_norm_g[l], k_norm_g[l])
        hist = state_pool[l].astype(p.dtype)
        a_out = pool_mixer(p, hist, PAST_LEN, pool_w[l], pool_scale[l])
        b_out, k_new, v_new = band_attention_sample(q, k, v, cache_k[l], cache_v[l], rel_bias[l])
        pool_s.append(jnp.concatenate([hist, p], axis=1)[:, -POOL_HIST:])
        ks_l.append(k_new)
        vs_l.append(v_new)
        xs = merge_and_ffn(xs, a_out, b_out, gates, w_branch_a[l], w_branch_b[l], w_out[l],
                           norm2_g[l], w_up[l], w_down[l])
    pool_prompt = jnp.stack(pool_p)
    k_prompt = jnp.stack(kp_l)
    v_prompt = jnp.stack(vp_l)
    pool_sample = jnp.stack(pool_s)
    k_sample = jnp.stack(ks_l)
    v_sample = jnp.stack(vs_l)
    return (xp, xs, pool_prompt, k_prompt, v_prompt, pool_sample, k_sample, v_sample)
```

```python
import numpy as np
from contextlib import ExitStack
import concourse.bass as bass
import concourse.mybir as mybir
from concourse.bass_utils import run_bass_kernel_spmd

F32 = mybir.dt.float32
BF16 = mybir.dt.bfloat16
AF = mybir.ActivationFunctionType
ALU = mybir.AluOpType

NCORES = 8
D = 2048
KC = 16
TT = 512
NS = 16
NM = TT + NS
H = 8
WIN = 2048
L = 2
EPS = 1e-6
SCALE = 128.0 ** -0.5
NSLOT = 4
WCOLS = 256
POOL_W = (2, 4, 8, 16)

PP_G1 = 0
PP_G2 = 32
PP_BG = 64
PP_PSC = 128
PP_QG = 144
PP_KG = 146
PP_INV = 148
NPP = 148 + 64


class Op:
    __slots__ = ("eng", "fn", "reads", "writes", "dkey", "deps", "sig", "cnt", "waits", "didx")


class Prog:
    def __init__(self):
        self.ops = []

    def add(self, eng, fn, reads=(), writes=(), dkey=None):
        o = Op()
        o.eng = eng
        o.fn = fn
        o.reads = tuple(reads)
        o.writes = tuple(writes)
        o.dkey = dkey
        self.ops.append(o)
        return o

    def analyze(self):
        last_w = {}
        readers = {}
        users = set()
        for i, o in enumerate(self.ops):
            deps = set()
            for r in o.reads:
                w = last_w.get(r)
                if w is not None:
                    deps.add(w)
            for w_ in o.writes:
                w = last_w.get(w_)
                if w is not None:
                    deps.add(w)
                for r in readers.get(w_, ()):
                    deps.add(r)
            deps.discard(i)
            if o.eng == "pe":
                deps = {d for d in deps if self.ops[d].eng != "pe" or self.ops[d].dkey is not None}
            o.deps = deps
            users.update(deps)
            for w_ in o.writes:
                last_w[w_] = i
                readers[w_] = []
            for r in o.reads:
                if r not in o.writes:
                    readers.setdefault(r, []).append(i)
        cnt = {}
        dcnt = {}
        for i, o in enumerate(self.ops):
            o.sig = False
            o.cnt = 0
            o.didx = 0
            if o.dkey is not None:
                dcnt[o.dkey] = dcnt.get(o.dkey, 0) + 1
                o.didx = dcnt[o.dkey]
                o.sig = True
            elif i in users:
                cnt[o.eng] = cnt.get(o.eng, 0) + 1
                o.cnt = cnt[o.eng]
                o.sig = True
        waited = {}
        for o in self.ops:
            need = {}
            for d in o.deps:
                p = self.ops[d]
                if p.dkey is not None:
                    key = ("d", p.dkey)
                    val = 16 * p.didx
                else:
                    key = ("e", p.eng)
                    val = p.cnt
                if val > need.get(key, 0):
                    need[key] = val
            ws = []
            wd = waited.setdefault(o.eng, {})
            for key, val in need.items():
                if wd.get(key, 0) >= val:
                    continue
                wd[key] = val
                ws.append((key, val))
            o.waits = ws
        self.dkeys = sorted(dcnt.keys(), key=str)


def build_program(STOP=99, WITH_SAMPLE=True, WITH_LAST=7):
    nc = bass.Bass("TRN2", target_bir_lowering=False)
    dt = nc.dram_tensor

    def din(name, shape):
        return dt(name, list(shape), F32, kind="ExternalInput").ap()

    def dout(name, shape):
        return dt(name, list(shape), F32, kind="ExternalOutput").ap()

    xw = din("xw", (WIN, D))
    xs = din("xs", (NS, D))
    sph = din("sph", (L, 15, 1024))
    ck = din("ck", (L, 512, 1024))
    cv = din("cv", (L, 512, 1024))
    w_in = din("w_in", (L, 1, 32, 128, 16, 256))
    pool_w = din("pool_w", (L, 4, 128, 2, 256))
    w_a = din("w_a", (L, 1, 8, 128, 8, 256))
    w_b = din("w_b", (L, 1, 8, 128, 8, 256))
    w_out = din("w_out", (L, 2, 4, 128, 8, 512))
    w_up = din("w_up", (L, 1, 32, 128, 16, 256))
    w_dn = din("w_dn", (L, 4, 8, 128, 16, 256))
    pp = din("pp", (128, NPP))
    ident = din("ident", (128, 128))
    vones = din("vones", (128, 2, 128))
    tb = din("tb", (L, H, 128, 640))
    tbs = din("tbs", (L, H, 128, 80))

    y_o = dout("y_o", (1024, D))
    ys_o = dout("ys_o", (NS, D))
    pool_o = dout("pool_o", (L, 15, 1024))
    k_o = dout("k_o", (L, 512, 1024))
    v_o = dout("v_o", (L, 512, 1024))
    pools_o = dout("pools_o", (L, 15, 1024))
    ks_o = dout("ks_o", (L, 512, 1024))
    vs_o = dout("vs_o", (L, 512, 1024))

    es = ExitStack()
    sb = lambda name, shape, dtype: es.enter_context(nc.sbuf_tensor(name, list(shape), dtype))
    xT = sb("xT", (128, KC, NM), F32)
    hT = sb("hT", (128, KC, NM), BF16)
    wsl = sb("wsl", (128, NSLOT, KC, WCOLS), BF16)
    KT = sb("KT", (128, L, H, 1024), BF16)
    Vc = sb("Vc", (128, L, 8, 1024), BF16)
    BIG = sb("BIG", (128, 32, NM), BF16)
    stg = sb("stg", (128, 2, 512), F32)
    SCRF = sb("SCRF", (128, 3264), F32)
    SCRB = sb("SCRB", (128, 2880), BF16)
    phist = sb("phist", (128, L, 8, 16), F32)
    phs = sb("phs", (128, L, 8, 16), F32)
    ppS = sb("ppS", (128, NPP), F32)
    identS = sb("identS", (128, 128), F32)
    vonesB = sb("vonesB", (128, 2, 128), BF16)
    KTn = sb("KTn", (128, H, NS), BF16)
    Vn = sb("Vn", (NS, 1024), BF16)
    dummy = sb("dmy", (128, 8), F32)
    ps = es.enter_context(nc.psum_tensor("ps", [128, 4096], F32))

    AT = lambda c: BIG[:, c, :]
    BTt = lambda c: BIG[:, 8 + c, :]
    qT = lambda h: BIG[:, 16 + h, :]
    mT = lambda c: BIG[:, 24 + c, :]
    uT = lambda j: BIG[:, j, :]
    idA = lambda c: ("big", c)
    idB = lambda c: ("big", 8 + c)
    idQ = lambda h: ("big", 16 + h)
    idM = lambda c: ("big", 24 + c)
    idU = lambda j: ("big", j)

    prog = Prog()
    A = prog.add
    SCRID = ("scrphase",)

    st = {"bank": 0, "ss": 0, "stg": 0, "rot": {}, "held": set()}

    def bank(avoid=()):
        b = st["bank"]
        while ("ps", b) in avoid or b in st["held"]:
            b = (b + 1) % 6
        st["bank"] = (b + 1) % 6
        return b

    def sslot():
        s = st["ss"]
        st["ss"] = (s + 1) % 2
        return 6 + s

    def sslot6():
        return 6

    def rot(name, n=2):
        r = st["rot"].get(name, 0)
        st["rot"][name] = (r + 1) % n
        return r

    def PSB(b, n=512, off=0):
        return ps[:, b * 512 + off: b * 512 + off + n]

    def PSS(s, parts=128):
        return ps[0:parts, s * 512: s * 512 + 16]

    class WS:
        def __init__(self):
            self.plan = []
            self.record = True
            self.pos = 0
            self.emitted = 0

        def _emit(self, i):
            src, nk, ncol = self.plan[i]
            slot = i % NSLOT
            A("pool", lambda e, src=src, slot=slot, nk=nk, ncol=ncol: e.dma_start(
                out=(wsl[:, slot, 0:nk, 0:ncol] if ncol == WCOLS else WIDE(slot)[:, 0:nk, :]), in_=src),
              reads=(), writes=(("w", slot),), dkey=("w", slot))

        def next(self, src, nk, ncol):
            if self.record:
                self.plan.append((src, nk, ncol))
                return 0
            i = self.pos
            self.pos += 1
            while self.emitted < min(len(self.plan), i + NSLOT):
                self._emit(self.emitted)
                self.emitted += 1
            return i % NSLOT

    ws = WS()

    def WIDE(slot):
        return wsl[:, slot].rearrange("p k n -> p (k n)").rearrange("p (k n) -> p k n", k=8)

    def wblock(wt, l, r0, nrows, c0, ncol=WCOLS):
        src = wt[l, r0 // nrows, c0 // ncol]
        return ws.next(src, nrows // 128, ncol)

    def parts(ns):
        return [(0, TT)] + ([(TT, TT + ns)] if ns else [])

    def gemm_fm(lhs_fn, lhs_ids, rhs_fn, rhs_ids, nk, ns):
        res = []
        b = bank()
        for k in range(nk):
            A("pe", lambda e, k=k, b=b: e.matmul(PSB(b), lhs_fn(k), rhs_fn(k)[:, 0:TT], start=(k == 0), stop=(k == nk - 1)),
              reads=tuple(lhs_ids(k)) + tuple(rhs_ids(k)), writes=(("ps", b),))
        res.append((PSB(b), 0, TT, ("ps", b)))
        if ns:
            s = sslot()
            for k in range(nk):
                A("pe", lambda e, k=k, s=s: e.matmul(PSS(s), lhs_fn(k), rhs_fn(k)[:, TT:TT + ns], start=(k == 0), stop=(k == nk - 1)),
                  reads=tuple(lhs_ids(k)) + tuple(rhs_ids(k)), writes=(("ps", s),))
            res.append((PSS(s), TT, TT + ns, ("ps", s)))
        return res

    def fence():
        A("dve", lambda e: e.memset(dummy[:, 0:1], 0.0), reads=(), writes=(SCRID,))

    def RSTD(r):
        return SCRF[:, r * NM:(r + 1) * NM]

    def SQ(r):
        return SCRB[:, r * NM:(r + 1) * NM]

    def rms_finish(ssparts, r, inv_n):
        for (pa, c0, c1, pid) in ssparts:
            A("act", lambda e, pa=pa, c0=c0, c1=c1: e.activation(out=RSTD(r)[:, c0:c1], in_=pa, func=AF.Ln, bias=EPS, scale=inv_n),
              reads=(pid, SCRID), writes=(("rstd", r),))
        n1 = ssparts[-1][2]
        A("act", lambda e: e.activation(out=RSTD(r)[:, 0:n1], in_=RSTD(r)[:, 0:n1], func=AF.Exp, scale=-0.5),
          reads=(("rstd", r), SCRID), writes=(("rstd", r),))

    def rmsnorm_stream(src_fn, src_ids, gcol_fn, dst_fn, dst_ids, nch, ns):
        n1 = TT + ns
        r = rot("rstd")
        b = bank()
        s = sslot() if ns else None
        for c in range(nch):
            q = rot("sq")
            A("act", lambda e, c=c, q=q: e.activation(out=SQ(q)[:, 0:n1], in_=src_fn(c)[:, 0:n1], func=AF.Square),
              reads=(src_ids(c), SCRID), writes=(("sq", q),))
            A("pe", lambda e, c=c, q=q: e.matmul(PSB(b), vonesB[:, 1, :], SQ(q)[:, 0:TT], start=(c == 0), stop=(c == nch - 1)),
              reads=(("sq", q), "vonesB"), writes=(("ps", b),))
            if ns:
                A("pe", lambda e, c=c, q=q: e.matmul(PSS(s), vonesB[:, 1, :], SQ(q)[:, TT:n1], start=(c == 0), stop=(c == nch - 1)),
                  reads=(("sq", q), "vonesB"), writes=(("ps", s),))
        ssp = [(PSB(b), 0, TT, ("ps", b))] + ([(PSS(s), TT, n1, ("ps", s))] if ns else [])
        rms_finish(ssp, r, 1.0 / (nch * 128))
        for c in range(nch):
            A("dve", lambda e, c=c: e.scalar_tensor_tensor(out=dst_fn(c)[:, 0:n1], in0=src_fn(c)[:, 0:n1], scalar=gcol_fn(c),
                                                           in1=RSTD(r)[:, 0:n1], op0=ALU.mult, op1=ALU.mult),
              reads=(src_ids(c), ("rstd", r), "ppS", SCRID), writes=(dst_ids(c),))

    def headnorm(gparts, gcol, outs, ns):
        n1 = TT + ns
        q = rot("sq")
        r = rot("rstd")
        for (pa, c0, c1, pid) in gparts:
            A("act", lambda e, pa=pa, c0=c0, c1=c1: e.activation(out=SQ(q)[:, c0:c1], in_=pa, func=AF.Square),
              reads=(pid, SCRID), writes=(("sq", q),))
        held = tuple(p[3] for p in gparts)
        b = bank(held)
        A("pe", lambda e: e.matmul(PSB(b), vonesB[:, 1, :], SQ(q)[:, 0:TT], start=True, stop=True),
          reads=(("sq", q), "vonesB"), writes=(("ps", b),))
        ssp = [(PSB(b), 0, TT, ("ps", b))]
        if ns:
            s = sslot()
            A("pe", lambda e: e.matmul(PSS(s), vonesB[:, 1, :], SQ(q)[:, TT:n1], start=True, stop=True),
              reads=(("sq", q), "vonesB"), writes=(("ps", s),))
            ssp.append((PSS(s), TT, n1, ("ps", s)))
        rms_finish(ssp, r, 1.0 / 128)
        for (pa, c0, c1, pid) in gparts:
            for (dfn, did) in outs:
                A("dve", lambda e, pa=pa, c0=c0, c1=c1, dfn=dfn: e.scalar_tensor_tensor(
                    out=dfn(c0, c1), in0=pa, scalar=gcol, in1=RSTD(r)[:, c0:c1], op0=ALU.mult, op1=ALU.mult),
                  reads=(pid, ("rstd", r), "ppS", SCRID), writes=(did,))

    def stg_slot():
        s = st["stg"]
        st["stg"] = (s + 1) % 2
        return s

    A("sp", lambda e: e.dma_start(out=ppS[:], in_=pp), writes=("ppS",), dkey="ppS")
    A("sp", lambda e: e.dma_start(out=identS[:], in_=ident), writes=("identS",), dkey="identS")
    A("sp", lambda e: e.dma_start(out=stg[:, 0, 0:256].rearrange("p (a n) -> p a n", a=2), in_=vones), writes=(("stg", 0),), dkey=("stg", 0))
    A("dve", lambda e: e.tensor_copy(out=vonesB[:], in_=stg[:, 0, 0:256].rearrange("p (a n) -> p a n", a=2)), reads=(("stg", 0),), writes=("vonesB",))
    A("dve", lambda e: e.memset(phist[:], 0.0), writes=tuple(("phist", l) for l in range(L)))
    A("dve", lambda e: e.memset(phs[:], 0.0), writes=tuple(("phs", l) for l in range(L)))
    A("dve", lambda e: e.memset(dummy[:], 0.0), writes=(SCRID,))

    def load_x(t, ns):
        for s4 in range(4):
            for fq in range(4):
                sl = stg_slot()
                r0 = t * TT + s4 * 128
                A("sp", lambda e, sl=sl, r0=r0, fq=fq: e.dma_start(out=stg[:, sl, :], in_=xw[r0:r0 + 128, fq * 512:(fq + 1) * 512]),
                  writes=(("stg", sl),), dkey=("stg", sl))
                b = bank()
                for cc in range(4):
                    A("pe", lambda e, sl=sl, cc=cc, b=b: e.transpose(out=PSB(b, 128, cc * 128), in_=stg[:, sl, cc * 128:(cc + 1) * 128], identity=identS[:]),
                      reads=(("stg", sl), "identS"), writes=(("ps", b),))
                eng = "act" if (fq % 2 == 0) else "dve"
                c0 = fq * 4
                if eng == "act":
                    A("act", lambda e, b=b, c0=c0, s4=s4: e.activation(out=xT[:, c0:c0 + 4, s4 * 128:(s4 + 1) * 128],
                                                                      in_=PSB(b).rearrange("p (a n) -> p a n", a=4), func=AF.Copy),
                      reads=(("ps", b),), writes=tuple(("x", c0 + i) for i in range(4)))
                else:
                    A("dve", lambda e, b=b, c0=c0, s4=s4: e.tensor_copy(out=xT[:, c0:c0 + 4, s4 * 128:(s4 + 1) * 128],
                                                                       in_=PSB(b).rearrange("p (a n) -> p a n", a=4)),
                      reads=(("ps", b),), writes=tuple(("x", c0 + i) for i in range(4)))
        if ns:
            for fq in range(4):
                sl = stg_slot()
                A("sp", lambda e, sl=sl, fq=fq: e.dma_start(out=stg[0:NS, sl, :], in_=xs[:, fq * 512:(fq + 1) * 512]),
                  writes=(("stg", sl),), dkey=("stg", sl))
                b = bank()
                for cc in range(4):
                    A("pe", lambda e, sl=sl, cc=cc, b=b: e.transpose(out=PSB(b, NS, cc * NS), in_=stg[0:NS, sl, cc * 128:(cc + 1) * 128], identity=identS[0:NS, 0:NS]),
                      reads=(("stg", sl), "identS"), writes=(("ps", b),))
                c0 = fq * 4
                A("dve", lambda e, b=b, c0=c0: e.tensor_copy(out=xT[:, c0:c0 + 4, TT:TT + NS],
                                                            in_=PSB(b, 4 * NS).rearrange("p (a n) -> p a n", a=4)),
                  reads=(("ps", b),), writes=tuple(("x", c0 + i) for i in range(4)))

    def store_y(t, ns):
        for s4 in range(4):
            for fq in range(4):
                b = bank()
                for cc in range(4):
                    c = fq * 4 + cc
                    A("pe", lambda e, c=c, cc=cc, b=b, s4=s4: e.transpose(out=PSB(b, 128, cc * 128), in_=xT[:, c, s4 * 128:(s4 + 1) * 128], identity=identS[:]),
                      reads=(("x", c), "identS"), writes=(("ps", b),))
                sl = stg_slot()
                A("act", lambda e, b=b, sl=sl: e.activation(out=stg[:, sl, :], in_=PSB(b), func=AF.Copy),
                  reads=(("ps", b),), writes=(("stg", sl),))
                r0 = (t - 2) * TT + s4 * 128
                A("sp", lambda e, sl=sl, r0=r0, fq=fq: e.dma_start(out=y_o[r0:r0 + 128, fq * 512:(fq + 1) * 512], in_=stg[:, sl, :]),
                  reads=(("stg", sl),), writes=(("OUT", "y", r0, fq),), dkey=("stg", sl))
        if ns:
            for fq in range(4):
                sl = stg_slot()
                for cc in range(4):
                    c = fq * 4 + cc
                    b = bank()
                    A("pe", lambda e, c=c, b=b: e.matmul(ps[0:NS, b * 512: b * 512 + 128], xT[:, c, TT:TT + NS], identS[:], start=True, stop=True),
                      reads=(("x", c), "identS"), writes=(("ps", b),))
                    A("act", lambda e, b=b, sl=sl, cc=cc: e.activation(out=stg[0:NS, sl, cc * 128:(cc + 1) * 128], in_=ps[0:NS, b * 512: b * 512 + 128], func=AF.Copy),
                      reads=(("ps", b),), writes=(("stg", sl),))
                A("sp", lambda e, sl=sl, fq=fq: e.dma_start(out=ys_o[:, fq * 512:(fq + 1) * 512], in_=stg[0:NS, sl, :]),
                  reads=(("stg", sl),), writes=(("OUT", "ys", fq),), dkey=("stg", sl))

    def run_pass(l, t, full, ns, last):
        n1 = TT + ns
        half = t % 2
        prev = 1 - half
        g1 = lambda c: ppS[:, PP_G1 + l * 16 + c: PP_G1 + l * 16 + c + 1]
        g2 = lambda c: ppS[:, PP_G2 + l * 16 + c: PP_G2 + l * 16 + c + 1]
        xid = lambda c: ("x", c)
        hid = lambda c: ("h", c)
        hids = lambda k: (("h", k),)

        fence()
        rmsnorm_stream(lambda c: xT[:, c, :], xid, g1, lambda c: hT[:, c, :], hid, KC, ns)

        pending = []

        use_defer = (ns == 0 and not last)

        def defer(fn):
            if not use_defer:
                fn()
                return
            pending.append(fn)
            if len(pending) > 1:
                pending.pop(0)()

        def flush():
            while pending:
                pending.pop(0)()

        PBUF = SCRF[:, 2 * NM: 2 * NM + 560]
        PTA = SCRF[:, 2 * NM + 560: 2 * NM + 1120]
        PTB = SCRF[:, 2 * NM + 1120: 2 * NM + 1680]
        KNF = lambda r: SCRF[:, 2 * NM + 1680 + r * NM: 2 * NM + 1680 + (r + 1) * NM]
        DT = lambda cc: SCRB[:, (2 + cc) * NM:(3 + cc) * NM]
        qg = ppS[:, PP_QG + l: PP_QG + l + 1]

        def q_block(bi):
            slot = wblock(w_in, l, 0, D, 1024 + bi * 256)
            for cc in range(2):
                h = bi * 2 + cc
                gp = gemm_fm(lambda k, cc=cc, slot=slot: wsl[:, slot, k, cc * 128:(cc + 1) * 128], lambda k, slot=slot: (("w", slot),),
                             lambda k: hT[:, k, :], hids, KC, ns)
                if use_defer:
                    st["held"].add(gp[0][3][1])

                def qfin(gp=gp, h=h):
                    headnorm(gp, qg, [(lambda c0, c1, h=h: qT(h)[:, c0:c1], idQ(h))], ns)
                    st["held"].discard(gp[0][3][1])
                defer(qfin)

        for gi in range(4):
            slot = wblock(w_in, l, 0, D, gi * 256)
            for cc in range(2):
                c = gi * 2 + cc
                if not full:
                    b = bank()
                    for k in range(KC):
                        A("pe", lambda e, k=k, b=b, cc=cc, slot=slot: e.matmul(PSB(b, 128), wsl[:, slot, k, cc * 128:(cc + 1) * 128], hT[:, k, TT - 128:TT],
                                                                                 start=(k == 0), stop=(k == KC - 1)),
                          reads=(("w", slot), ("h", k)), writes=(("ps", b),))
                    A("act", lambda e, b=b, c=c: e.activation(out=phist[:, l, c, :], in_=PSB(b, 128)[:, 112:128], func=AF.Copy),
                      reads=(("ps", b),), writes=(("phist", l),))
                    continue
                gp = gemm_fm(lambda k, cc=cc, slot=slot: wsl[:, slot, k, cc * 128:(cc + 1) * 128], lambda k, slot=slot: (("w", slot),),
                             lambda k: hT[:, k, :], hids, KC, ns)
                A("dve", lambda e, c=c: e.tensor_copy(out=PBUF[:, 0:16], in_=phist[:, l, c, :]),
                  reads=(("phist", l), SCRID), writes=("pbuf",))
                A("act", lambda e, pa=gp[0][0]: e.activation(out=PBUF[:, 16:16 + TT], in_=pa, func=AF.Copy),
                  reads=(gp[0][3], SCRID), writes=("pbuf",))
                if ns:
                    A("dve", lambda e, c=c: e.tensor_copy(out=PBUF[:, 528:544], in_=phs[:, l, c, :]),
                      reads=(("phs", l), SCRID), writes=("pbuf",))
                    A("act", lambda e, pa=gp[1][0]: e.activation(out=PBUF[:, 544:560], in_=pa, func=AF.Copy),
                      reads=(gp[1][3], SCRID), writes=("pbuf",))
                A("dve", lambda e, c=c: e.tensor_copy(out=phist[:, l, c, :], in_=PBUF[:, TT:TT + 16]),
                  reads=("pbuf", SCRID), writes=(("phist", l),))
                if last and (int(WITH_LAST) & 1):
                    tail_out(l, c, PBUF, ns)
                w = POOL_W[gi]
                segs = [(0, 16 + TT)] + ([(528, 560)] if ns else [])
                cur, cid = PBUF, "pbuf"
                tmps = [(PTA, "pta"), (PTB, "ptb")]
                sh = 1
                k = 0
                while sh < w:
                    nxt, nid = tmps[k % 2]
                    for (a0, a1) in segs:
                        A("dve", lambda e, cur=cur, nxt=nxt, a0=a0, a1=a1, sh=sh: e.tensor_tensor(
                            out=nxt[:, a0 + 2 * sh - 1:a1], in0=cur[:, a0 + 2 * sh - 1:a1], in1=cur[:, a0 + sh - 1:a1 - sh], op=ALU.add),
                          reads=(cid, SCRID), writes=(nid,))
                    cur, cid = nxt, nid
                    sh *= 2
                    k += 1
                for (a0, a1), (c0, c1) in zip(segs, parts(ns)):
                    A("dve", lambda e, cur=cur, a0=a0, a1=a1, c0=c0, c1=c1, cc=cc, w=w: e.scalar_tensor_tensor(
                        out=DT(cc)[:, c0:c1], in0=cur[:, a0 + 16:a1], scalar=1.0 / w, in1=PBUF[:, a0 + 16:a1],
                        op0=ALU.mult, op1=ALU.subtract),
                      reads=(cid, "pbuf", SCRID), writes=(("dT", cc),))
                if t == 2:
                    inv = ppS[:, PP_INV + gi * 16: PP_INV + gi * 16 + 16]
                    A("dve", lambda e, cur=cur, inv=inv: e.tensor_tensor(out=PTA[:, 0:16] if cur is not PTA else PTB[:, 0:16],
                                                                        in0=cur[:, 16:32], in1=inv, op=ALU.mult),
                      reads=(cid, "ppS", SCRID), writes=("pta" if cur is not PTA else "ptb",))
                    A("dve", lambda e, cur=cur, cc=cc: e.tensor_tensor(out=DT(cc)[:, 0:16], in0=PTA[:, 0:16] if cur is not PTA else PTB[:, 0:16],
                                                                      in1=PBUF[:, 16:32], op=ALU.subtract),
                      reads=("pta" if cur is not PTA else "ptb", "pbuf", SCRID), writes=(("dT", cc),))
            if full:
                if use_defer:
                    q_block(gi)
                    flush()
                pws = wblock_pool(l, gi)
                for oc in range(2):
                    c = gi * 2 + oc
                    gp = gemm_fm(lambda k, oc=oc, pws=pws: wsl[:, pws, k, oc * 128:(oc + 1) * 128], lambda k, pws=pws: (("w", pws),),
                                 lambda k: DT(k), lambda k: (("dT", k), SCRID), 2, ns)
                    psc = ppS[:, PP_PSC + l * 8 + c: PP_PSC + l * 8 + c + 1]
                    for (pa, c0, c1, pid) in gp:
                        A("act", lambda e, pa=pa, c0=c0, c1=c1, c=c, psc=psc: e.activation(out=AT(c)[:, c0:c1], in_=pa, func=AF.Copy, scale=psc),
                          reads=(pid, "ppS"), writes=(idA(c),))
        if full and not use_defer:
            for bi in range(4):
                q_block(bi)
        kg = ppS[:, PP_KG + l: PP_KG + l + 1]
        for bi in range(4):
            slot = wblock(w_in, l, 0, D, 2048 + bi * 256)
            for cc in range(2):
                h = bi * 2 + cc
                gp = gemm_fm(lambda k, cc=cc, slot=slot: wsl[:, slot, k, cc * 128:(cc + 1) * 128], lambda k, slot=slot: (("w", slot),),
                             lambda k: hT[:, k, :], hids, KC, ns)
                if use_defer:
                    st["held"].add(gp[0][3][1])

                def kfin(gp=gp, h=h):
                    st["held"].discard(gp[0][3][1])
                    if not last:
                        headnorm(gp, kg, [(lambda c0, c1, h=h: KT[:, l, h, half * 512 + c0: half * 512 + c1], ("KT", l, half, h))], 0)
                    else:
                        r = rot("knf", 1)
                        headnorm(gp, kg, [(lambda c0, c1, r=r: KNF(r)[:, c0:c1], ("knf", r))], ns)
                        A("act", lambda e, r=r, h=h: e.activation(out=KT[:, l, h, half * 512: half * 512 + TT], in_=KNF(r)[:, 0:TT], func=AF.Copy),
                          reads=(("knf", r), SCRID), writes=(("KT", l, half, h),))
                        if ns:
                            A("act", lambda e, r=r, h=h: e.activation(out=KTn[:, h, :], in_=KNF(r)[:, TT:TT + NS], func=AF.Copy),
                              reads=(("knf", r), SCRID), writes=(("KTn", h),))
                        if int(WITH_LAST) & 2:
                            kout(l, h, KNF(r), ("knf", r), ns)
                defer(kfin)
        flush()
        for bi in range(4):
            slot = wblock(w_in, l, 0, D, 3072 + bi * 256)
            for sp2 in range(2):
                b = bank()
                for s2 in range(2):
                    s4 = sp2 * 2 + s2
                    for k in range(KC):
                        A("pe", lambda e, k=k, s4=s4, s2=s2, b=b, slot=slot: e.matmul(PSB(b, 256, s2 * 256), hT[:, k, s4 * 128:(s4 + 1) * 128], wsl[:, slot, k, :],
                                                                                       start=(k == 0), stop=(k == KC - 1)),
                          reads=(("h", k), ("w", slot)), writes=(("ps", b),))
                v0 = half * 4 + sp2 * 2
                A("act", lambda e, b=b, v0=v0, bi=bi: e.activation(out=Vc[:, l, v0:v0 + 2, bi * 256:(bi + 1) * 256],
                                                                  in_=PSB(b).rearrange("p (a n) -> p a n", a=2), func=AF.Copy),
                  reads=(("ps", b),), writes=(("Vc", l, half, sp2 * 2), ("Vc", l, half, sp2 * 2 + 1)))
                if last and (int(WITH_LAST) & 4):
                    sl = stg_slot()
                    A("dve", lambda e, b=b, sl=sl: e.tensor_copy(out=stg[:, sl, :], in_=PSB(b)),
                      reads=(("ps", b), ("Vc", l, half, sp2 * 2)), writes=(("stg", sl),))
                    r0 = sp2 * 256
                    A("sp", lambda e, sl=sl, r0=r0, bi=bi: e.dma_start(
                        out=v_o[l, r0:r0 + 256, bi * 256:(bi + 1) * 256].rearrange("(a p) n -> p a n", p=128),
                        in_=stg[:, sl, :].rearrange("p (a n) -> p a n", a=2)),
                      reads=(("stg", sl),), writes=(("OUT", "v", l, sp2, bi),), dkey=("stg", sl))
            if ns:
                b = bank()
                for k in range(KC):
                    A("pe", lambda e, k=k, b=b, slot=slot: e.matmul(ps[0:NS, b * 512: b * 512 + 256], hT[:, k, TT:TT + NS], wsl[:, slot, k, :],
                                                                     start=(k == 0), stop=(k == KC - 1)),
                      reads=(("h", k), ("w", slot)), writes=(("ps", b),))
                A("act", lambda e, b=b, bi=bi: e.activation(out=Vn[:, bi * 256:(bi + 1) * 256], in_=ps[0:NS, b * 512: b * 512 + 256], func=AF.Copy),
                  reads=(("ps", b),), writes=("Vn",))
                sl = stg_slot()
                A("dve", lambda e, b=b, sl=sl: e.tensor_copy(out=stg[0:NS, sl, 0:256], in_=ps[0:NS, b * 512: b * 512 + 256]),
                  reads=(("ps", b), "Vn"), writes=(("stg", sl),))
                A("sp", lambda e, sl=sl, bi=bi: e.dma_start(out=vs_o[l, 512 - NS:512, bi * 256:(bi + 1) * 256], in_=stg[0:NS, sl, 0:256]),
                  reads=(("stg", sl),), writes=(("OUT", "vs", l, bi),), dkey=("stg", sl))
        if not full:
            return

        fence()
        TBST = lambda r: SCRF[:, r * 640:(r + 1) * 640]
        EE = lambda r: SCRF[:, 1280 + r * 640: 1280 + (r + 1) * 640]
        RD = lambda r: SCRF[:, 2560 + r * 128: 2560 + (r + 1) * 128]
        TBSS = lambda r: SCRF[:, 2816 + r * 80: 2816 + (r + 1) * 80]
        ES = lambda r: SCRF[:, 2976 + r * 80: 2976 + (r + 1) * 80]
        EXPB = lambda r: SCRB[:, r * 640:(r + 1) * 640]
        PM = lambda r: SCRB[:, 1280 + r * 640: 1280 + (r + 1) * 640]
        EXPBS = lambda r: SCRB[:, 2560 + r * 80: 2560 + (r + 1) * 80]
        PMS = lambda r: SCRB[:, 2720 + r * 80: 2720 + (r + 1) * 80]

        def keyinfo(m):
            hf = prev if m < 4 else half
            wt = (t - 1) if m < 4 else t
            return hf, m % 4, (0 if wt < 2 else 1)

        iters = [(h, qt) for h in range(H) for qt in range(4)]
        pend = {}

        def emit_S(i):
            h, qt = iters[i]
            if qt == 0:
                rb = rot("tb")
                pend["rb"] = rb
                A("sp", lambda e, rb=rb, h=h: e.dma_start(out=TBST(rb), in_=tb[l, h]), reads=(SCRID,), writes=(("tbst", rb),), dkey=("tbst", rb))
                A("act", lambda e, rb=rb: e.activation(out=EXPB(rb), in_=TBST(rb), func=AF.Exp),
                  reads=(("tbst", rb), SCRID), writes=(("expb", rb),))
                A("dve", lambda e, rb=rb: e.memset(EXPB(rb)[0:64, 64:128], 0.0), reads=(SCRID,), writes=(("expb", rb),))
                A("dve", lambda e, rb=rb: e.memset(EXPB(rb)[64:128, 512:576], 0.0), reads=(SCRID,), writes=(("expb", rb),))
            rb = pend["rb"]
            bp = 2 * (i % 2)
            for j in range(5):
                hf, kk, vi = keyinfo(qt + j)
                A("pe", lambda e, j=j, hf=hf, kk=kk, bp=bp, h=h, qt=qt: e.matmul(
                    ps[:, bp * 512 + j * 128: bp * 512 + (j + 1) * 128], KT[:, l, h, hf * 512 + kk * 128: hf * 512 + (kk + 1) * 128],
                    qT(h)[:, qt * 128:(qt + 1) * 128], start=True, stop=True),
                  reads=(("KT", l, hf, h), idQ(h)), writes=(("ps", bp if j < 4 else bp + 1),))
            re_ = i % 2
            A("act", lambda e, bp=bp, re_=re_: e.activation(out=EE(re_), in_=ps[:, bp * 512: bp * 512 + 640], func=AF.Exp, scale=SCALE),
              reads=(("ps", bp), ("ps", bp + 1), SCRID), writes=(("E", re_),))
            rp = i % 2
            A("pool", lambda e, re_=re_, rp=rp, rb=rb: e.tensor_tensor(out=PM(rp)[:, 0:384], in0=EE(re_)[:, 0:384], in1=EXPB(rb)[:, 0:384], op=ALU.mult),
              reads=(("E", re_), ("expb", rb), SCRID), writes=(("pm", rp, 0),))
            A("dve", lambda e, re_=re_, rp=rp, rb=rb: e.tensor_tensor(out=PM(rp)[:, 384:640], in0=EE(re_)[:, 384:640], in1=EXPB(rb)[:, 384:640], op=ALU.mult),
              reads=(("E", re_), ("expb", rb), SCRID), writes=(("pm", rp, 1),))

        def emit_PV(i):
            h, qt = iters[i]
            rp = i % 2
            bo = 4 if i % 2 == 0 else 6
            bd = 5 if i % 2 == 0 else 7
            for j in range(5):
                hf, kk, vi = keyinfo(qt + j)
                A("pe", lambda e, j=j, hf=hf, kk=kk, bo=bo, rp=rp, h=h: e.matmul(
                    PSB(bo, 128), Vc[:, l, hf * 4 + kk, h * 128:(h + 1) * 128], PM(rp)[:, j * 128:(j + 1) * 128], start=(j == 0), stop=(j == 4)),
                  reads=(("Vc", l, hf, kk), ("pm", rp, 0 if j < 3 else 1)), writes=(("ps", bo),))
            for j in range(5):
                hf, kk, vi = keyinfo(qt + j)
                A("pe", lambda e, j=j, vi=vi, rp=rp, bd=bd: e.matmul(PSB(bd, 128), vonesB[:, vi, :], PM(rp)[:, j * 128:(j + 1) * 128], start=(j == 0), stop=(j == 4)),
                  reads=("vonesB", ("pm", rp, 0 if j < 3 else 1)), writes=(("ps", bd),))
            rd = i % 2
            A("dve", lambda e, rd=rd, bd=bd: e.reciprocal(out=RD(rd), in_=PSB(bd, 128)),
              reads=(("ps", bd), SCRID), writes=(("rd", rd),))
            A("dve", lambda e, rd=rd, bo=bo, h=h, qt=qt: e.tensor_tensor(out=BTt(h)[:, qt * 128:(qt + 1) * 128], in0=PSB(bo, 128), in1=RD(rd), op=ALU.mult),
              reads=(("ps", bo), ("rd", rd), SCRID), writes=(idB(h),))

        emit_S(0)
        for i in range(len(iters)):
            if i + 1 < len(iters):
                emit_S(i + 1)
            emit_PV(i)
        if ns:
            sample_attention(l, prev, TBSS, ES, EXPBS, PMS, RD)

        fence()
        GA = lambda r: SCRF[:, r * NM:(r + 1) * NM]
        GB = lambda r: SCRF[:, (2 + r) * NM:(3 + r) * NM]
        T1 = lambda r: SCRF[:, (4 + r) * NM:(5 + r) * NM]
        for mh in range(2):
            for cp4 in range(4):
                cp = mh * 4 + cp4
                sga = wblock(w_in, l, 0, D, 4096 + cp * 256)
                gas = []
                for cc in range(2):
                    c = cp * 2 + cc
                    gp = gemm_fm(lambda k, cc=cc, sga=sga: wsl[:, sga, k, cc * 128:(cc + 1) * 128], lambda k, sga=sga: (("w", sga),),
                                 lambda k: hT[:, k, :], hids, KC, ns)
                    r = rot("ga")
                    bgc = ppS[:, PP_BG + l * 32 + c: PP_BG + l * 32 + c + 1]
                    for (pa, c0, c1, pid) in gp:
                        A("act", lambda e, pa=pa, c0=c0, c1=c1, r=r, bgc=bgc: e.activation(out=GA(r)[:, c0:c1], in_=pa, func=AF.Sigmoid, bias=bgc),
                          reads=(pid, "ppS", SCRID), writes=(("ga", r),))
                    gas.append(r)
                sgb = wblock(w_in, l, 0, D, 6144 + cp * 256)
                gbs = []
                for cc in range(2):
                    c = cp * 2 + cc
                    gp = gemm_fm(lambda k, cc=cc, sgb=sgb: wsl[:, sgb, k, cc * 128:(cc + 1) * 128], lambda k, sgb=sgb: (("w", sgb),),
                                 lambda k: hT[:, k, :], hids, KC, ns)
                    r = rot("gb")
                    bgc = ppS[:, PP_BG + l * 32 + 16 + c: PP_BG + l * 32 + 16 + c + 1]
                    for (pa, c0, c1, pid) in gp:
                        A("act", lambda e, pa=pa, c0=c0, c1=c1, r=r, bgc=bgc: e.activation(out=GB(r)[:, c0:c1], in_=pa, func=AF.Sigmoid, bias=bgc),
                          reads=(pid, "ppS", SCRID), writes=(("gb", r),))
                    gbs.append(r)
                swa = wblock(w_a, l, 0, 1024, cp * 256)
                t1s = []
                for cc in range(2):
                    gp = gemm_fm(lambda k, cc=cc, swa=swa: wsl[:, swa, k, cc * 128:(cc + 1) * 128], lambda k, swa=swa: (("w", swa),),
                                 lambda k: AT(k), lambda k: (idA(k),), 8, ns)
                    r = rot("t1")
                    for (pa, c0, c1, pid) in gp:
                        A("dve", lambda e, pa=pa, c0=c0, c1=c1, r=r, g=gas[cc]: e.tensor_tensor(out=T1(r)[:, c0:c1], in0=pa, in1=GA(g)[:, c0:c1], op=ALU.mult),
                          reads=(pid, ("ga", gas[cc]), SCRID), writes=(("t1", r),))
                    t1s.append(r)
                swb = wblock(w_b, l, 0, 1024, cp * 256)
                for cc in range(2):
                    c4 = cp4 * 2 + cc
                    gp = gemm_fm(lambda k, cc=cc, swb=swb: wsl[:, swb, k, cc * 128:(cc + 1) * 128], lambda k, swb=swb: (("w", swb),),
                                 lambda k: BTt(k), lambda k: (idB(k),), 8, ns)
                    for (pa, c0, c1, pid) in gp:
                        A("dve", lambda e, pa=pa, c0=c0, c1=c1, g=gbs[cc]: e.tensor_tensor(out=GB(g)[:, c0:c1], in0=pa, in1=GB(g)[:, c0:c1], op=ALU.mult),
                          reads=(pid, ("gb", gbs[cc]), SCRID), writes=(("gb", gbs[cc]),))
                        A("dve", lambda e, c0=c0, c1=c1, r=t1s[cc], c4=c4, g=gbs[cc]: e.tensor_tensor(out=mT(c4)[:, c0:c1], in0=T1(r)[:, c0:c1], in1=GB(g)[:, c0:c1], op=ALU.add),
                          reads=(("gb", gbs[cc]), ("t1", t1s[cc]), SCRID), writes=(idM(c4),))
            for og in range(4):
                so = wblock(w_out, l, mh * 1024, 1024, og * 512, 512)
                for cc in range(4):
                    o = og * 4 + cc
                    gp = gemm_fm(lambda k, cc=cc, so=so: WIDE(so)[:, k, cc * 128:(cc + 1) * 128], lambda k, so=so: (("w", so),),
                                 lambda k: mT(k), lambda k: (idM(k),), 8, ns)
                    for (pa, c0, c1, pid) in gp:
                        A("dve", lambda e, pa=pa, c0=c0, c1=c1, o=o: e.tensor_tensor(out=xT[:, o, c0:c1], in0=pa, in1=xT[:, o, c0:c1], op=ALU.add),
                          reads=(pid, ("x", o)), writes=(("x", o),))

        fence()
        rmsnorm_stream(lambda c: xT[:, c, :], xid, g2, lambda c: hT[:, c, :], hid, KC, ns)

        RT = lambda r: SCRF[:, 2 * NM + r * NM: 2 * NM + (r + 1) * NM]
        for hf2 in range(2):
            for ub in range(16):
                su = wblock(w_up, l, 0, D, hf2 * 4096 + ub * 256)
                for cc in range(2):
                    j = ub * 2 + cc
                    gp = gemm_fm(lambda k, cc=cc, su=su: wsl[:, su, k, cc * 128:(cc + 1) * 128], lambda k, su=su: (("w", su),),
                                 lambda k: hT[:, k, :], hids, KC, ns)
                    r = rot("rt")
                    for (pa, c0, c1, pid) in gp:
                        A("act", lambda e, pa=pa, c0=c0, c1=c1, r=r: e.activation(out=RT(r)[:, c0:c1], in_=pa, func=AF.Relu),
                          reads=(pid, SCRID), writes=(("rt", r),))
                    A("dve", lambda e, r=r, j=j: e.tensor_tensor(out=uT(j)[:, 0:n1], in0=RT(r)[:, 0:n1], in1=RT(r)[:, 0:n1], op=ALU.mult),
                      reads=(("rt", r), SCRID), writes=(idU(j),))
            for og in range(8):
                bks = [bank(), bank()]
                sss = [6, 7] if ns else None
                for r2 in range(2):
                    sdl = wblock(w_dn, l, hf2 * 4096 + r2 * 2048, 2048, og * 256)
                    for cc in range(2):
                        for k in range(16):
                            kk = r2 * 16 + k
                            A("pe", lambda e, k=k, kk=kk, cc=cc, sdl=sdl, b=bks[cc]: e.matmul(
                                PSB(b), wsl[:, sdl, k, cc * 128:(cc + 1) * 128], uT(kk)[:, 0:TT], start=(kk == 0), stop=(kk == 31)),
                              reads=(("w", sdl), idU(kk)), writes=(("ps", bks[cc]),))
                        if ns:
                            for k in range(16):
                                kk = r2 * 16 + k
                                A("pe", lambda e, k=k, kk=kk, cc=cc, sdl=sdl, s_=sss[cc]: e.matmul(
                                    PSS(s_), wsl[:, sdl, k, cc * 128:(cc + 1) * 128], uT(kk)[:, TT:TT + ns], start=(kk == 0), stop=(kk == 31)),
                                  reads=(("w", sdl), idU(kk)), writes=(("ps", sss[cc]),))
                for cc in range(2):
                    o = og * 2 + cc
                    gp = [(PSB(bks[cc]), 0, TT, ("ps", bks[cc]))] + ([(PSS(sss[cc]), TT, TT + ns, ("ps", sss[cc]))] if ns else [])
                    for (pa, c0, c1, pid) in gp:
                        A("dve", lambda e, pa=pa, c0=c0, c1=c1, o=o: e.tensor_tensor(out=xT[:, o, c0:c1], in0=pa, in1=xT[:, o, c0:c1], op=ALU.add),
                          reads=(pid, ("x", o)), writes=(("x", o),))

    def wblock_pool(l, gi):
        src = pool_w[l, gi]
        return ws.next(src, 2, 256)

    def tail_out(l, c, PBUF, ns):
        for (a0, dst, tag) in [(TT, pool_o, "pool")] + ([(544, pools_o, "pools")] if ns else []):
            b = bank()
            A("pe", lambda e, b=b, a0=a0: e.matmul(ps[0:16, b * 512: b * 512 + 128], PBUF[:, a0:a0 + 16], identS[:], start=True, stop=True),
              reads=("pbuf", "identS", SCRID), writes=(("ps", b),))
            sl = stg_slot()
            A("act", lambda e, b=b, sl=sl: e.activation(out=stg[0:16, sl, 0:128], in_=ps[0:16, b * 512: b * 512 + 128], func=AF.Copy),
              reads=(("ps", b),), writes=(("stg", sl),))
            A("sp", lambda e, sl=sl, dst=dst, c=c: e.dma_start(out=dst[l, :, c * 128:(c + 1) * 128], in_=stg[1:16, sl, 0:128]),
              reads=(("stg", sl),), writes=(("OUT", tag, l, c),), dkey=("stg", sl))

    def kout(l, h, knf, knfid, ns):
        b = bank()
        for s4 in range(4):
            A("pe", lambda e, b=b, s4=s4: e.transpose(out=PSB(b, 128, s4 * 128), in_=knf[:, s4 * 128:(s4 + 1) * 128], identity=identS[:]),
              reads=(knfid, "identS", SCRID), writes=(("ps", b),))
        sl = stg_slot()
        A("act", lambda e, b=b, sl=sl: e.activation(out=stg[:, sl, :], in_=PSB(b), func=AF.Copy),
          reads=(("ps", b),), writes=(("stg", sl),))
        A("sp", lambda e, sl=sl: e.dma_start(out=k_o[l, :, h * 128:(h + 1) * 128].rearrange("(a p) n -> p a n", p=128),
                                             in_=stg[:, sl, :].rearrange("p (a n) -> p a n", a=4)),
          reads=(("stg", sl),), writes=(("OUT", "k", l, h),), dkey=("stg", sl))
        if ns:
            b = bank()
            A("pe", lambda e, b=b: e.matmul(ps[0:NS, b * 512: b * 512 + 128], knf[:, TT:TT + NS], identS[:], start=True, stop=True),
              reads=(knfid, "identS", SCRID), writes=(("ps", b),))
            sl = stg_slot()
            A("act", lambda e, b=b, sl=sl: e.activation(out=stg[0:NS, sl, 0:128], in_=ps[0:NS, b * 512: b * 512 + 128], func=AF.Copy),
              reads=(("ps", b),), writes=(("stg", sl),))
            A("sp", lambda e, sl=sl: e.dma_start(out=ks_o[l, 512 - NS:512, h * 128:(h + 1) * 128], in_=stg[0:NS, sl, 0:128]),
              reads=(("stg", sl),), writes=(("OUT", "ks", l, h),), dkey=("stg", sl))

    def sample_attention(l, prev, TBSS, ES, EXPBS, PMS, RD):
        for j in range(4):
            for fh in range(2):
                sl = stg_slot()
                A("sp", lambda e, sl=sl, j=j, fh=fh: e.dma_start(out=stg[:, sl, :], in_=ck[l, j * 128:(j + 1) * 128, fh * 512:(fh + 1) * 512]),
                  writes=(("stg", sl),), dkey=("stg", sl))
                b = bank()
                for hh in range(4):
                    A("pe", lambda e, sl=sl, hh=hh, b=b: e.transpose(out=PSB(b, 128, hh * 128), in_=stg[:, sl, hh * 128:(hh + 1) * 128], identity=identS[:]),
                      reads=(("stg", sl), "identS"), writes=(("ps", b),))
                h0 = fh * 4
                A("act", lambda e, b=b, h0=h0, j=j: e.activation(out=KT[:, l, h0:h0 + 4, prev * 512 + j * 128: prev * 512 + (j + 1) * 128],
                                                                in_=PSB(b).rearrange("p (a n) -> p a n", a=4), func=AF.Copy),
                  reads=(("ps", b),), writes=tuple(("KT", l, prev, h0 + i) for i in range(4)))
                sl = stg_slot()
                A("sp", lambda e, sl=sl, j=j, fh=fh: e.dma_start(out=stg[:, sl, :], in_=cv[l, j * 128:(j + 1) * 128, fh * 512:(fh + 1) * 512]),
                  writes=(("stg", sl),), dkey=("stg", sl))
                A("dve", lambda e, sl=sl, j=j, fh=fh: e.tensor_copy(out=Vc[:, l, prev * 4 + j, fh * 512:(fh + 1) * 512], in_=stg[:, sl, :]),
                  reads=(("stg", sl),), writes=(("Vc", l, prev, j),))
        for h in range(H):
            rb = rot("tbs")
            A("sp", lambda e, rb=rb, h=h: e.dma_start(out=TBSS(rb), in_=tbs[l, h]), reads=(SCRID,), writes=(("tbss", rb),), dkey=("tbss", rb))
            A("act", lambda e, rb=rb: e.activation(out=EXPBS(rb), in_=TBSS(rb), func=AF.Exp),
              reads=(("tbss", rb), SCRID), writes=(("expbs", rb),))
            for j in range(4):
                A("pe", lambda e, j=j, h=h: e.matmul(ps[:, j * NS:(j + 1) * NS], KT[:, l, h, prev * 512 + j * 128: prev * 512 + (j + 1) * 128],
                                                     qT(h)[:, TT:TT + NS], start=True, stop=True),
                  reads=(("KT", l, prev, h), idQ(h)), writes=(("ps", 0),))
            A("pe", lambda e, h=h: e.matmul(ps[0:NS, 4 * NS:5 * NS], KTn[:, h, :], qT(h)[:, TT:TT + NS], start=True, stop=True),
              reads=(("KTn", h), idQ(h)), writes=(("ps", 0),))
            re_ = rot("es")
            A("act", lambda e, re_=re_: e.activation(out=ES(re_)[:, 0:64], in_=ps[:, 0:64], func=AF.Exp, scale=SCALE),
              reads=(("ps", 0), SCRID), writes=(("es", re_),))
            A("act", lambda e, re_=re_: e.activation(out=ES(re_)[0:NS, 64:80], in_=ps[0:NS, 64:80], func=AF.Exp, scale=SCALE),
              reads=(("ps", 0), SCRID), writes=(("es", re_),))
            rp = rot("pms")
            A("dve", lambda e, re_=re_, rp=rp, rb=rb: e.tensor_tensor(out=PMS(rp)[:, 0:64], in0=ES(re_)[:, 0:64], in1=EXPBS(rb)[:, 0:64], op=ALU.mult),
              reads=(("es", re_), ("expbs", rb), SCRID), writes=(("pms", rp),))
            A("dve", lambda e, re_=re_, rp=rp, rb=rb: e.tensor_tensor(out=PMS(rp)[0:NS, 64:80], in0=ES(re_)[0:NS, 64:80], in1=EXPBS(rb)[0:NS, 64:80], op=ALU.mult),
              reads=(("es", re_), ("expbs", rb), SCRID), writes=(("pms", rp),))
            for j in range(4):
                A("pe", lambda e, j=j, rp=rp, h=h: e.matmul(PSB(4, NS), Vc[:, l, prev * 4 + j, h * 128:(h + 1) * 128], PMS(rp)[:, j * NS:(j + 1) * NS],
                                                            start=(j == 0), stop=False),
                  reads=(("Vc", l, prev, j), ("pms", rp)), writes=(("ps", 4),))
            A("pe", lambda e, rp=rp, h=h: e.matmul(PSB(4, NS), Vn[:, h * 128:(h + 1) * 128], PMS(rp)[0:NS, 64:80], start=False, stop=True),
              reads=("Vn", ("pms", rp)), writes=(("ps", 4),))
            for j in range(4):
                A("pe", lambda e, j=j, rp=rp: e.matmul(PSB(5, NS), vonesB[:, 1, :], PMS(rp)[:, j * NS:(j + 1) * NS], start=(j == 0), stop=False),
                  reads=("vonesB", ("pms", rp)), writes=(("ps", 5),))
            A("pe", lambda e, rp=rp: e.matmul(PSB(5, NS), vonesB[0:NS, 1, :], PMS(rp)[0:NS, 64:80], start=False, stop=True),
              reads=("vonesB", ("pms", rp)), writes=(("ps", 5),))
            rd = rot("rd")
            A("dve", lambda e, rd=rd: e.reciprocal(out=RD(rd)[:, 0:NS], in_=PSB(5, NS)),
              reads=(("ps", 5), SCRID), writes=(("rd", rd),))
            A("dve", lambda e, rd=rd, h=h: e.tensor_tensor(out=BTt(h)[:, TT:TT + NS], in0=PSB(4, NS), in1=RD(rd)[:, 0:NS], op=ALU.mult),
              reads=(("ps", 4), ("rd", rd), SCRID), writes=(idB(h),))

    def sample_setup():
        for l in range(L):
            for fh in range(2):
                sl = stg_slot()
                A("sp", lambda e, sl=sl, l=l, fh=fh: e.dma_start(out=stg[0:15, sl, :], in_=sph[l, :, fh * 512:(fh + 1) * 512]),
                  writes=(("stg", sl),), dkey=("stg", sl))
                b = bank()
                for cc in range(4):
                    A("pe", lambda e, sl=sl, cc=cc, b=b: e.transpose(out=PSB(b, 15, cc * 16 + 1), in_=stg[0:15, sl, cc * 128:(cc + 1) * 128], identity=identS[0:15, 0:15]),
                      reads=(("stg", sl), "identS"), writes=(("ps", b),))
                A("dve", lambda e, b=b, l=l, fh=fh: e.tensor_copy(out=phs[:, l, fh * 4:fh * 4 + 4, 1:16],
                                                                 in_=PSB(b, 64).rearrange("p (a n) -> p a n", a=4)[:, :, 1:16]),
                  reads=(("ps", b),), writes=(("phs", l),))
            A("sp", lambda e, l=l: e.dma_start(out=ks_o[l, 0:512 - NS, :], in_=ck[l, NS:512, :]), writes=(("OUT", "ksc", l),), dkey=("ksc", l))
            A("sp", lambda e, l=l: e.dma_start(out=vs_o[l, 0:512 - NS, :], in_=cv[l, NS:512, :]), writes=(("OUT", "vsc", l),), dkey=("vsc", l))

    def body(with_sample=WITH_SAMPLE):
        stop = STOP
        if stop >= 1:
            sample_setup()
        for t in range(4):
            wsm = int(with_sample) if with_sample is not True else 7
            ns0 = NS if (t == 3 and (wsm & 1)) else 0
            ns1 = NS if (t == 3 and (wsm & 2)) else 0
            nsy = NS if (t == 3 and (wsm & 4)) else 0
            last = (t == 3) and bool(WITH_LAST)
            if stop >= 2 + 3 * t:
                load_x(t, NS if (t == 3 and wsm) else 0)
            if stop >= 3 + 3 * t:
                run_pass(0, t, full=(t >= 1), ns=ns0, last=last)
            if t >= 1 and stop >= 4 + 3 * t:
                run_pass(1, t, full=(t >= 2), ns=ns1, last=last)
                if t >= 2:
                    store_y(t, nsy)
        outs = set()
        for o in prog.ops:
            for w_ in o.writes:
                if isinstance(w_, tuple) and w_ and w_[0] == "OUT":
                    outs.add(w_)
        A("sp", None, reads=tuple(sorted(outs, key=str)), writes=())

    ws.record = True
    body()
    plan = ws.plan
    prog.ops = []
    st.update({"bank": 0, "ss": 0, "ss6": 0, "stg": 0, "rot": {}, "held": set()})
    ws.record = False
    ws.pos = 0
    ws.emitted = 0
    A("sp", lambda e: e.dma_start(out=ppS[:], in_=pp), writes=("ppS",), dkey="ppS")
    A("sp", lambda e: e.dma_start(out=identS[:], in_=ident), writes=("identS",), dkey="identS")
    A("sp", lambda e: e.dma_start(out=stg[:, 0, 0:256].rearrange("p (a n) -> p a n", a=2), in_=vones), writes=(("stg", 0),), dkey=("stg", 0))
    A("dve", lambda e: e.tensor_copy(out=vonesB[:], in_=stg[:, 0, 0:256].rearrange("p (a n) -> p a n", a=2)), reads=(("stg", 0),), writes=("vonesB",))
    A("dve", lambda e: e.memset(phist[:], 0.0), writes=tuple(("phist", l) for l in range(L)))
    A("dve", lambda e: e.memset(phs[:], 0.0), writes=tuple(("phs", l) for l in range(L)))
    A("dve", lambda e: e.memset(dummy[:], 0.0), writes=(SCRID,))
    body()
    assert ws.pos == len(plan), (ws.pos, len(plan))

    prog.analyze()
    sems = {}
    for en in ("pe", "act", "dve", "pool", "sp"):
        sems[("e", en)] = es.enter_context(nc.semaphore("s_" + en))
    for i, dk in enumerate(prog.dkeys):
        sems[("d", dk)] = es.enter_context(nc.semaphore("d%d" % i))
    block = es.enter_context(nc.Block())
    per = {en: [o for o in prog.ops if o.eng == en] for en in ("pe", "act", "dve", "pool", "sp")}

    def emit(en, e):
        for o in per[en]:
            for (key, val) in o.waits:
                e.wait_ge(sems[key], val)
            if o.fn is None:
                continue
            ins = o.fn(e)
            if o.sig:
                if o.dkey is not None:
                    ins.then_inc(sems[("d", o.dkey)], 16)
                else:
                    ins.then_inc(sems[("e", en)], 1)

    @block.tensor
    def _(e):
        emit("pe", e)

    @block.scalar
    def _(e):
        emit("act", e)

    @block.vector
    def _(e):
        emit("dve", e)

    @block.gpsimd
    def _(e):
        emit("pool", e)

    @block.sync
    def _(e):
        emit("sp", e)

    es.close()
    return nc, len(prog.ops)


def _host_layout(inp):
    f = lambda a: np.ascontiguousarray(np.asarray(a, dtype=np.float32))
    x_prompt = f(inp["x_prompt"]); x_sample = f(inp["x_sample"])
    state_pool = f(inp["state_pool"]); cache_k = f(inp["cache_k"]); cache_v = f(inp["cache_v"])
    rel_bias = f(inp["rel_bias"])
    def tile_w(w, rb, cb):
        l_, r_, c_ = w.shape
        t = w.reshape(l_, r_ // rb, rb // 128, 128, c_ // cb, cb)
        return np.ascontiguousarray(t.transpose(0, 1, 4, 3, 2, 5))

    shared = {
        "w_in": tile_w(f(inp["w_in"]), 2048, 256),
        "pool_w": np.ascontiguousarray(f(inp["pool_w"]).reshape(L, 4, 2, 128, 256).transpose(0, 1, 3, 2, 4)),
        "w_a": tile_w(f(inp["w_branch_a"]), 1024, 256), "w_b": tile_w(f(inp["w_branch_b"]), 1024, 256),
        "w_out": tile_w(f(inp["w_out"]), 1024, 512), "w_up": tile_w(f(inp["w_up"]), 2048, 256),
        "w_dn": tile_w(f(inp["w_down"]), 2048, 256),
        "ident": np.eye(128, dtype=np.float32),
    }
    kj = np.arange(640)[:, None]
    qi = np.arange(128)[None, :]
    idx = np.clip(qi - kj + 512, -128, 128) + 128
    tbf = rel_bias[:, :, idx]
    tbf = tbf.reshape(L, H, 5, 128, 128).transpose(0, 1, 3, 2, 4).reshape(L, H, 128, 640)
    shared["tb"] = np.ascontiguousarray(tbf)
    kpos = np.concatenate([512 + np.arange(512), 1024 + np.arange(16), np.full(112, 1024)])
    qpos = 1024 + np.arange(16)
    idxs = np.clip(qpos[None, :] - kpos[:, None], -128, 128) + 128
    tbsf = rel_bias[:, :, idxs].reshape(L, H, 5, 128, 16).transpose(0, 1, 3, 2, 4).reshape(L, H, 128, 80)
    shared["tbs"] = np.ascontiguousarray(tbsf)
    ppb = np.zeros((128, NPP), np.float32)
    for l in range(L):
        ppb[:, PP_G1 + l * 16: PP_G1 + (l + 1) * 16] = f(inp["norm1_g"])[l].reshape(16, 128).T
        ppb[:, PP_G2 + l * 16: PP_G2 + (l + 1) * 16] = f(inp["norm2_g"])[l].reshape(16, 128).T
        ppb[:, PP_BG + l * 32: PP_BG + (l + 1) * 32] = f(inp["b_gate"])[l].reshape(32, 128).T
        ppb[:, PP_PSC + l * 8: PP_PSC + (l + 1) * 8] = f(inp["pool_scale"])[l].reshape(8, 128).T
        ppb[:, PP_QG + l] = f(inp["q_norm_g"])[l]
        ppb[:, PP_KG + l] = f(inp["k_norm_g"])[l]
    in_maps = []
    for c in range(NCORES):
        b, q = divmod(c, 4)
        xw = np.zeros((WIN, D), np.float32)
        s0 = q * 1024 - 1024
        lo = max(s0, 0)
        xw[lo - s0:] = x_prompt[b, lo:(q + 1) * 1024]
        ppc = ppb.copy()
        for gi, w in enumerate(POOL_W):
            pos = q * 1024 + np.arange(16)
            ppc[:, PP_INV + gi * 16: PP_INV + (gi + 1) * 16] = (1.0 / np.minimum(pos + 1, w)).astype(np.float32)[None, :]
        vo = np.ones((128, 2, 128), np.float32)
        if q == 0:
            vo[:, 0, :] = 1e-30
        m = dict(shared)
        m.update({
            "xw": xw, "xs": x_sample[c], "sph": state_pool[:, c], "ck": cache_k[:, c].reshape(L, 512, 1024),
            "cv": cache_v[:, c].reshape(L, 512, 1024), "pp": ppc, "vones": vo,
        })
        in_maps.append(m)
    return in_maps


_CACHE = {}
_STOP = [99]


def kernel(**inputs):
    if "nc" not in _CACHE:
        _CACHE["nc"] = build_program(*_STOP)
    nc, _ = _CACHE["nc"]
    in_maps = _host_layout(inputs)
    res = run_bass_kernel_spmd(nc, in_maps, core_ids=list(range(NCORES)))
    r = res.results
    y_prompt = np.zeros((2, 4096, D), np.float32)
    for c in range(NCORES):
        b, q = divmod(c, 4)
        y_prompt[b, q * 1024:(q + 1) * 1024] = r[c]["y_o"]
    y_sample = np.stack([r[c]["ys_o"] for c in range(NCORES)])
    last = [3, 7]
    pool_prompt = np.stack([r[c]["pool_o"] for c in last], axis=1)
    k_prompt = np.stack([r[c]["k_o"] for c in last], axis=1).reshape(L, 2, 512, H, 128)
    v_prompt = np.stack([r[c]["v_o"] for c in last], axis=1).reshape(L, 2, 512, H, 128)
    pool_sample = np.stack([r[c]["pools_o"] for c in range(NCORES)], axis=1)
    k_sample = np.stack([r[c]["ks_o"] for c in range(NCORES)], axis=1).reshape(L, NCORES, 512, H, 128)
    v_sample = np.stack([r[c]["vs_o"] for c in range(NCORES)], axis=1).reshape(L, NCORES, 512, H, 128)
    f = lambda a: np.ascontiguousarray(a, dtype=np.float32)
    return (f(y_prompt), f(y_sample), f(pool_prompt), f(k_prompt), f(v_prompt), f(pool_sample), f(k_sample), f(v_sample))
```

```python
import numpy as np
from contextlib import ExitStack
import concourse.bass as bass
import concourse.mybir as mybir
from concourse.bass_utils import run_bass_kernel_spmd

F32 = mybir.dt.float32
BF16 = mybir.dt.bfloat16
AF = mybir.ActivationFunctionType
ALU = mybir.AluOpType

NCORES = 8
D = 2048
KC = 16
TT = 512
NS = 16
NM = TT + NS
H = 8
WIN = 2048
L = 2
EPS = 1e-6
SCALE = 128.0 ** -0.5
NSLOT = 4
WCOLS = 256
POOL_W = (2, 4, 8, 16)

PP_G1 = 0
PP_G2 = 32
PP_BG = 64
PP_PSC = 128
PP_QG = 144
PP_KG = 146
PP_INV = 148
NPP = 148 + 64


class Op:
    __slots__ = ("eng", "fn", "reads", "writes", "dkey", "deps", "sig", "cnt", "waits", "didx")


class Prog:
    def __init__(self):
        self.ops = []

    def add(self, eng, fn, reads=(), writes=(), dkey=None):
        o = Op()
        o.eng = eng
        o.fn = fn
        o.reads = tuple(reads)
        o.writes = tuple(writes)
        o.dkey = dkey
        self.ops.append(o)
        return o

    def analyze(self):
        last_w = {}
        readers = {}
        users = set()
        for i, o in enumerate(self.ops):
            deps = set()
            for r in o.reads:
                w = last_w.get(r)
                if w is not None:
                    deps.add(w)
            for w_ in o.writes:
                w = last_w.get(w_)
                if w is not None:
                    deps.add(w)
                for r in readers.get(w_, ()):
                    deps.add(r)
            deps.discard(i)
            if o.eng == "pe":
                deps = {d for d in deps if self.ops[d].eng != "pe" or self.ops[d].dkey is not None}
            o.deps = deps
            users.update(deps)
            for w_ in o.writes:
                last_w[w_] = i
                readers[w_] = []
            for r in o.reads:
                if r not in o.writes:
                    readers.setdefault(r, []).append(i)
        cnt = {}
        dcnt = {}
        for i, o in enumerate(self.ops):
            o.sig = False
            o.cnt = 0
            o.didx = 0
            if o.dkey is not None:
                dcnt[o.dkey] = dcnt.get(o.dkey, 0) + 1
                o.didx = dcnt[o.dkey]
                o.sig = True
            elif i in users:
                cnt[o.eng] = cnt.get(o.eng, 0) + 1
                o.cnt = cnt[o.eng]
                o.sig = True
        waited = {}
        for o in self.ops:
            need = {}
            for d in o.deps:
                p = self.ops[d]
                if p.dkey is not None:
                    key = ("d", p.dkey)
                    val = 16 * p.didx
                else:
                    key = ("e", p.eng)
                    val = p.cnt
                if val > need.get(key, 0):
                    need[key] = val
            ws = []
            wd = waited.setdefault(o.eng, {})
            for key, val in need.items():
                if wd.get(key, 0) >= val:
                    continue
                wd[key] = val
                ws.append((key, val))
            o.waits = ws
        self.dkeys = sorted(dcnt.keys(), key=str)


def build_program(STOP=99, WITH_SAMPLE=True, WITH_LAST=7):
    nc = bass.Bass("TRN2", target_bir_lowering=False)
    dt = nc.dram_tensor

    def din(name, shape):
        return dt(name, list(shape), F32, kind="ExternalInput").ap()

    def dout(name, shape):
        return dt(name, list(shape), F32, kind="ExternalOutput").ap()

    xw = din("xw", (WIN, D))
    xs = din("xs", (NS, D))
    sph = din("sph", (L, 15, 1024))
    ck = din("ck", (L, 512, 1024))
    cv = din("cv", (L, 512, 1024))
    w_in = din("w_in", (L, 1, 32, 128, 16, 256))
    pool_w = din("pool_w", (L, 4, 128, 2, 256))
    w_a = din("w_a", (L, 1, 8, 128, 8, 256))
    w_b = din("w_b", (L, 1, 8, 128, 8, 256))
    w_out = din("w_out", (L, 2, 4, 128, 8, 512))
    w_up = din("w_up", (L, 1, 32, 128, 16, 256))
    w_dn = din("w_dn", (L, 4, 8, 128, 16, 256))
    pp = din("pp", (128, NPP))
    ident = din("ident", (128, 128))
    vones = din("vones", (128, 2, 128))
    tb = din("tb", (L, H, 128, 640))
    tbs = din("tbs", (L, H, 128, 80))

    y_o = dout("y_o", (1024, D))
    ys_o = dout("ys_o", (NS, D))
    pool_o = dout("pool_o", (L, 15, 1024))
    k_o = dout("k_o", (L, 512, 1024))
    v_o = dout("v_o", (L, 512, 1024))
    pools_o = dout("pools_o", (L, 15, 1024))
    ks_o = dout("ks_o", (L, 512, 1024))
    vs_o = dout("vs_o", (L, 512, 1024))

    es = ExitStack()
    sb = lambda name, shape, dtype: es.enter_context(nc.sbuf_tensor(name, list(shape), dtype))
    xT = sb("xT", (128, KC, NM), F32)
    hT = sb("hT", (128, KC, NM), BF16)
    wsl = sb("wsl", (128, NSLOT, KC, WCOLS), BF16)
    KT = sb("KT", (128, L, H, 1024), BF16)
    Vc = sb("Vc", (128, L, 8, 1024), BF16)
    BIG = sb("BIG", (128, 32, NM), BF16)
    stg = sb("stg", (128, 2, 512), F32)
    SCRF = sb("SCRF", (128, 3264), F32)
    SCRB = sb("SCRB", (128, 2880), BF16)
    phist = sb("phist", (128, L, 8, 16), F32)
    phs = sb("phs", (128, L, 8, 16), F32)
    ppS = sb("ppS", (128, NPP), F32)
    identS = sb("identS", (128, 128), F32)
    vonesB = sb("vonesB", (128, 2, 128), BF16)
    KTn = sb("KTn", (128, H, NS), BF16)
    Vn = sb("Vn", (NS, 1024), BF16)
    dummy = sb("dmy", (128, 8), F32)
    ps = es.enter_context(nc.psum_tensor("ps", [128, 4096], F32))

    AT = lambda c: BIG[:, c, :]
    BTt = lambda c: BIG[:, 8 + c, :]
    qT = lambda h: BIG[:, 16 + h, :]
    mT = lambda c: BIG[:, 24 + c, :]
    uT = lambda j: BIG[:, j, :]
    idA = lambda c: ("big", c)
    idB = lambda c: ("big", 8 + c)
    idQ = lambda h: ("big", 16 + h)
    idM = lambda c: ("big", 24 + c)
    idU = lambda j: ("big", j)

    prog = Prog()
    A = prog.add
    SCRID = ("scrphase",)

    st = {"bank": 0, "ss": 0, "stg": 0, "rot": {}, "held": set()}

    def bank(avoid=()):
        b = st["bank"]
        while ("ps", b) in avoid or b in st["held"]:
            b = (b + 1) % 6
        st["bank"] = (b + 1) % 6
        return b

    def sslot():
        s = st["ss"]
        st["ss"] = (s + 1) % 2
        return 6 + s

    def sslot6():
        return 6

    def rot(name, n=2):
        r = st["rot"].get(name, 0)
        st["rot"][name] = (r + 1) % n
        return r

    def PSB(b, n=512, off=0):
        return ps[:, b * 512 + off: b * 512 + off + n]

    def PSS(s, parts=128):
        return ps[0:parts, s * 512: s * 512 + 16]

    class WS:
        def __init__(self):
            self.plan = []
            self.record = True
            self.pos = 0
            self.emitted = 0

        def _emit(self, i):
            src, nk, ncol = self.plan[i]
            slot = i % NSLOT
            A("pool", lambda e, src=src, slot=slot, nk=nk, ncol=ncol: e.dma_start(
                out=(wsl[:, slot, 0:nk, 0:ncol] if ncol == WCOLS else WIDE(slot)[:, 0:nk, :]), in_=src),
              reads=(), writes=(("w", slot),), dkey=("w", slot))

        def next(self, src, nk, ncol):
            if self.record:
                self.plan.append((src, nk, ncol))
                return 0
            i = self.pos
            self.pos += 1
            while self.emitted < min(len(self.plan), i + NSLOT):
                self._emit(self.emitted)
                self.emitted += 1
            return i % NSLOT

    ws = WS()

    def WIDE(slot):
        return wsl[:, slot].rearrange("p k n -> p (k n)").rearrange("p (k n) -> p k n", k=8)

    def wblock(wt, l, r0, nrows, c0, ncol=WCOLS):
        src = wt[l, r0 // nrows, c0 // ncol]
        return ws.next(src, nrows // 128, ncol)

    def parts(ns):
        return [(0, TT)] + ([(TT, TT + ns)] if ns else [])

    def gemm_fm(lhs_fn, lhs_ids, rhs_fn, rhs_ids, nk, ns):
        res = []
        b = bank()
        for k in range(nk):
            A("pe", lambda e, k=k, b=b: e.matmul(PSB(b), lhs_fn(k), rhs_fn(k)[:, 0:TT], start=(k == 0), stop=(k == nk - 1)),
              reads=tuple(lhs_ids(k)) + tuple(rhs_ids(k)), writes=(("ps", b),))
        res.append((PSB(b), 0, TT, ("ps", b)))
        if ns:
            s = sslot()
            for k in range(nk):
                A("pe", lambda e, k=k, s=s: e.matmul(PSS(s), lhs_fn(k), rhs_fn(k)[:, TT:TT + ns], start=(k == 0), stop=(k == nk - 1)),
                  reads=tuple(lhs_ids(k)) + tuple(rhs_ids(k)), writes=(("ps", s),))
            res.append((PSS(s), TT, TT + ns, ("ps", s)))
        return res

    def fence():
        A("dve", lambda e: e.memset(dummy[:, 0:1], 0.0), reads=(), writes=(SCRID,))

    def RSTD(r):
        return SCRF[:, r * NM:(r + 1) * NM]

    def SQ(r):
        return SCRB[:, r * NM:(r + 1) * NM]

    def rms_finish(ssparts, r, inv_n):
        for (pa, c0, c1, pid) in ssparts:
            A("act", lambda e, pa=pa, c0=c0, c1=c1: e.activation(out=RSTD(r)[:, c0:c1], in_=pa, func=AF.Ln, bias=EPS, scale=inv_n),
              reads=(pid, SCRID), writes=(("rstd", r),))
        n1 = ssparts[-1][2]
        A("act", lambda e: e.activation(out=RSTD(r)[:, 0:n1], in_=RSTD(r)[:, 0:n1], func=AF.Exp, scale=-0.5),
          reads=(("rstd", r), SCRID), writes=(("rstd", r),))

    def rmsnorm_stream(src_fn, src_ids, gcol_fn, dst_fn, dst_ids, nch, ns):
        n1 = TT + ns
        r = rot("rstd")
        b = bank()
        s = sslot() if ns else None
        for c in range(nch):
            q = rot("sq")
            A("act", lambda e, c=c, q=q: e.activation(out=SQ(q)[:, 0:n1], in_=src_fn(c)[:, 0:n1], func=AF.Square),
              reads=(src_ids(c), SCRID), writes=(("sq", q),))
            A("pe", lambda e, c=c, q=q: e.matmul(PSB(b), vonesB[:, 1, :], SQ(q)[:, 0:TT], start=(c == 0), stop=(c == nch - 1)),
              reads=(("sq", q), "vonesB"), writes=(("ps", b),))
            if ns:
                A("pe", lambda e, c=c, q=q: e.matmul(PSS(s), vonesB[:, 1, :], SQ(q)[:, TT:n1], start=(c == 0), stop=(c == nch - 1)),
                  reads=(("sq", q), "vonesB"), writes=(("ps", s),))
        ssp = [(PSB(b), 0, TT, ("ps", b))] + ([(PSS(s), TT, n1, ("ps", s))] if ns else [])
        rms_finish(ssp, r, 1.0 / (nch * 128))
        for c in range(nch):
            A("dve", lambda e, c=c: e.scalar_tensor_tensor(out=dst_fn(c)[:, 0:n1], in0=src_fn(c)[:, 0:n1], scalar=gcol_fn(c),
                                                           in1=RSTD(r)[:, 0:n1], op0=ALU.mult, op1=ALU.mult),
              reads=(src_ids(c), ("rstd", r), "ppS", SCRID), writes=(dst_ids(c),))

    def headnorm(gparts, gcol, outs, ns):
        n1 = TT + ns
        q = rot("sq")
        r = rot("rstd")
        for (pa, c0, c1, pid) in gparts:
            A("act", lambda e, pa=pa, c0=c0, c1=c1: e.activation(out=SQ(q)[:, c0:c1], in_=pa, func=AF.Square),
              reads=(pid, SCRID), writes=(("sq", q),))
        held = tuple(p[3] for p in gparts)
        b = bank(held)
        A("pe", lambda e: e.matmul(PSB(b), vonesB[:, 1, :], SQ(q)[:, 0:TT], start=True, stop=True),
          reads=(("sq", q), "vonesB"), writes=(("ps", b),))
        ssp = [(PSB(b), 0, TT, ("ps", b))]
        if ns:
            s = sslot()
            A("pe", lambda e: e.matmul(PSS(s), vonesB[:, 1, :], SQ(q)[:, TT:n1], start=True, stop=True),
              reads=(("sq", q), "vonesB"), writes=(("ps", s),))
            ssp.append((PSS(s), TT, n1, ("ps", s)))
        rms_finish(ssp, r, 1.0 / 128)
        for (pa, c0, c1, pid) in gparts:
            for (dfn, did) in outs:
                A("dve", lambda e, pa=pa, c0=c0, c1=c1, dfn=dfn: e.scalar_tensor_tensor(
                    out=dfn(c0, c1), in0=pa, scalar=gcol, in1=RSTD(r)[:, c0:c1], op0=ALU.mult, op1=ALU.mult),
                  reads=(pid, ("rstd", r), "ppS", SCRID), writes=(did,))

    def stg_slot():
        s = st["stg"]
        st["stg"] = (s + 1) % 2
        return s

    A("sp", lambda e: e.dma_start(out=ppS[:], in_=pp), writes=("ppS",), dkey="ppS")
    A("sp", lambda e: e.dma_start(out=identS[:], in_=ident), writes=("identS",), dkey="identS")
    A("sp", lambda e: e.dma_start(out=stg[:, 0, 0:256].rearrange("p (a n) -> p a n", a=2), in_=vones), writes=(("stg", 0),), dkey=("stg", 0))
    A("dve", lambda e: e.tensor_copy(out=vonesB[:], in_=stg[:, 0, 0:256].rearrange("p (a n) -> p a n", a=2)), reads=(("stg", 0),), writes=("vonesB",))
    A("dve", lambda e: e.memset(phist[:], 0.0), writes=tuple(("phist", l) for l in range(L)))
    A("dve", lambda e: e.memset(phs[:], 0.0), writes=tuple(("phs", l) for l in range(L)))
    A("dve", lambda e: e.memset(dummy[:], 0.0), writes=(SCRID,))

    def load_x(t, ns):
        for fq in range(4):
            for s4 in range(4):
                sl = stg_slot()
                r0 = t * TT + s4 * 128
                A("sp", lambda e, sl=sl, r0=r0, fq=fq: e.dma_start(out=stg[:, sl, :], in_=xw[r0:r0 + 128, fq * 512:(fq + 1) * 512]),
                  writes=(("stg", sl),), dkey=("stg", sl))
                b = bank()
                for cc in range(4):
                    A("pe", lambda e, sl=sl, cc=cc, b=b: e.transpose(out=PSB(b, 128, cc * 128), in_=stg[:, sl, cc * 128:(cc + 1) * 128], identity=identS[:]),
                      reads=(("stg", sl), "identS"), writes=(("ps", b),))
                eng = "act" if (s4 % 2 == 0) else "dve"
                c0 = fq * 4
                if eng == "act":
                    A("act", lambda e, b=b, c0=c0, s4=s4: e.activation(out=xT[:, c0:c0 + 4, s4 * 128:(s4 + 1) * 128],
                                                                      in_=PSB(b).rearrange("p (a n) -> p a n", a=4), func=AF.Copy),
                      reads=(("ps", b),), writes=tuple(("x", c0 + i) for i in range(4)))
                else:
                    A("dve", lambda e, b=b, c0=c0, s4=s4: e.tensor_copy(out=xT[:, c0:c0 + 4, s4 * 128:(s4 + 1) * 128],
                                                                       in_=PSB(b).rearrange("p (a n) -> p a n", a=4)),
                      reads=(("ps", b),), writes=tuple(("x", c0 + i) for i in range(4)))
        if ns:
            for fq in range(4):
                sl = stg_slot()
                A("sp", lambda e, sl=sl, fq=fq: e.dma_start(out=stg[0:NS, sl, :], in_=xs[:, fq * 512:(fq + 1) * 512]),
                  writes=(("stg", sl),), dkey=("stg", sl))
                b = bank()
                for cc in range(4):
                    A("pe", lambda e, sl=sl, cc=cc, b=b: e.transpose(out=PSB(b, NS, cc * NS), in_=stg[0:NS, sl, cc * 128:(cc + 1) * 128], identity=identS[0:NS, 0:NS]),
                      reads=(("stg", sl), "identS"), writes=(("ps", b),))
                c0 = fq * 4
                A("dve", lambda e, b=b, c0=c0: e.tensor_copy(out=xT[:, c0:c0 + 4, TT:TT + NS],
                                                            in_=PSB(b, 4 * NS).rearrange("p (a n) -> p a n", a=4)),
                  reads=(("ps", b),), writes=tuple(("x", c0 + i) for i in range(4)))

    def store_y(t, ns):
        for s4 in range(4):
            for fq in range(4):
                b = bank()
                for cc in range(4):
                    c = fq * 4 + cc
                    A("pe", lambda e, c=c, cc=cc, b=b, s4=s4: e.transpose(out=PSB(b, 128, cc * 128), in_=xT[:, c, s4 * 128:(s4 + 1) * 128], identity=identS[:]),
                      reads=(("x", c), "identS"), writes=(("ps", b),))
                sl = stg_slot()
                A("act", lambda e, b=b, sl=sl: e.activation(out=stg[:, sl, :], in_=PSB(b), func=AF.Copy),
                  reads=(("ps", b),), writes=(("stg", sl),))
                r0 = (t - 2) * TT + s4 * 128
                A("sp", lambda e, sl=sl, r0=r0, fq=fq: e.dma_start(out=y_o[r0:r0 + 128, fq * 512:(fq + 1) * 512], in_=stg[:, sl, :]),
                  reads=(("stg", sl),), writes=(("OUT", "y", r0, fq),), dkey=("stg", sl))
        if ns:
            for fq in range(4):
                sl = stg_slot()
                for cc in range(4):
                    c = fq * 4 + cc
                    b = bank()
                    A("pe", lambda e, c=c, b=b: e.matmul(ps[0:NS, b * 512: b * 512 + 128], xT[:, c, TT:TT + NS], identS[:], start=True, stop=True),
                      reads=(("x", c), "identS"), writes=(("ps", b),))
                    A("act", lambda e, b=b, sl=sl, cc=cc: e.activation(out=stg[0:NS, sl, cc * 128:(cc + 1) * 128], in_=ps[0:NS, b * 512: b * 512 + 128], func=AF.Copy),
                      reads=(("ps", b),), writes=(("stg", sl),))
                A("sp", lambda e, sl=sl, fq=fq: e.dma_start(out=ys_o[:, fq * 512:(fq + 1) * 512], in_=stg[0:NS, sl, :]),
                  reads=(("stg", sl),), writes=(("OUT", "ys", fq),), dkey=("stg", sl))

    def run_pass(l, t, full, ns, last):
        n1 = TT + ns
        half = t % 2
        prev = 1 - half
        g1 = lambda c: ppS[:, PP_G1 + l * 16 + c: PP_G1 + l * 16 + c + 1]
        g2 = lambda c: ppS[:, PP_G2 + l * 16 + c: PP_G2 + l * 16 + c + 1]
        xid = lambda c: ("x", c)
        hid = lambda c: ("h", c)
        hids = lambda k: (("h", k),)

        fence()
        rmsnorm_stream(lambda c: xT[:, c, :], xid, g1, lambda c: hT[:, c, :], hid, KC, ns)

        pending = []

        use_defer = (ns == 0 and not last)

        def defer(fn):
            if not use_defer:
                fn()
                return
            pending.append(fn)
            if len(pending) > 1:
                pending.pop(0)()

        def flush():
            while pending:
                pending.pop(0)()

        PBUF = SCRF[:, 2 * NM: 2 * NM + 560]
        PTA = SCRF[:, 2 * NM + 560: 2 * NM + 1120]
        PTB = SCRF[:, 2 * NM + 1120: 2 * NM + 1680]
        KNF = lambda r: SCRF[:, 2 * NM + 1680 + r * NM: 2 * NM + 1680 + (r + 1) * NM]
        DT = lambda cc: SCRB[:, (2 + cc) * NM:(3 + cc) * NM]
        qg = ppS[:, PP_QG + l: PP_QG + l + 1]

        def q_block(bi):
            slot = wblock(w_in, l, 0, D, 1024 + bi * 256)
            for cc in range(2):
                h = bi * 2 + cc
                gp = gemm_fm(lambda k, cc=cc, slot=slot: wsl[:, slot, k, cc * 128:(cc + 1) * 128], lambda k, slot=slot: (("w", slot),),
                             lambda k: hT[:, k, :], hids, KC, ns)
                if use_defer:
                    st["held"].add(gp[0][3][1])

                def qfin(gp=gp, h=h):
                    headnorm(gp, qg, [(lambda c0, c1, h=h: qT(h)[:, c0:c1], idQ(h))], ns)
                    st["held"].discard(gp[0][3][1])
                defer(qfin)

        for gi in range(4):
            slot = wblock(w_in, l, 0, D, gi * 256)
            for cc in range(2):
                c = gi * 2 + cc
                if not full:
                    b = bank()
                    for k in range(KC):
                        A("pe", lambda e, k=k, b=b, cc=cc, slot=slot: e.matmul(PSB(b, 128), wsl[:, slot, k, cc * 128:(cc + 1) * 128], hT[:, k, TT - 128:TT],
                                                                                 start=(k == 0), stop=(k == KC - 1)),
                          reads=(("w", slot), ("h", k)), writes=(("ps", b),))
                    A("act", lambda e, b=b, c=c: e.activation(out=phist[:, l, c, :], in_=PSB(b, 128)[:, 112:128], func=AF.Copy),
                      reads=(("ps", b),), writes=(("phist", l),))
                    continue
                gp = gemm_fm(lambda k, cc=cc, slot=slot: wsl[:, slot, k, cc * 128:(cc + 1) * 128], lambda k, slot=slot: (("w", slot),),
                             lambda k: hT[:, k, :], hids, KC, ns)
                A("dve", lambda e, c=c: e.tensor_copy(out=PBUF[:, 0:16], in_=phist[:, l, c, :]),
                  reads=(("phist", l), SCRID), writes=("pbuf",))
                A("act", lambda e, pa=gp[0][0]: e.activation(out=PBUF[:, 16:16 + TT], in_=pa, func=AF.Copy),
                  reads=(gp[0][3], SCRID), writes=("pbuf",))
                if ns:
                    A("dve", lambda e, c=c: e.tensor_copy(out=PBUF[:, 528:544], in_=phs[:, l, c, :]),
                      reads=(("phs", l), SCRID), writes=("pbuf",))
                    A("act", lambda e, pa=gp[1][0]: e.activation(out=PBUF[:, 544:560], in_=pa, func=AF.Copy),
                      reads=(gp[1][3], SCRID), writes=("pbuf",))
                A("dve", lambda e, c=c: e.tensor_copy(out=phist[:, l, c, :], in_=PBUF[:, TT:TT + 16]),
                  reads=("pbuf", SCRID), writes=(("phist", l),))
                if last and (int(WITH_LAST) & 1):
                    tail_out(l, c, PBUF, ns)
                w = POOL_W[gi]
                segs = [(0, 16 + TT)] + ([(528, 560)] if ns else [])
                cur, cid = PBUF, "pbuf"
                tmps = [(PTA, "pta"), (PTB, "ptb")]
                sh = 1
                k = 0
                while sh < w:
                    nxt, nid = tmps[k % 2]
                    for (a0, a1) in segs:
                        A("dve", lambda e, cur=cur, nxt=nxt, a0=a0, a1=a1, sh=sh: e.tensor_tensor(
                            out=nxt[:, a0 + 2 * sh - 1:a1], in0=cur[:, a0 + 2 * sh - 1:a1], in1=cur[:, a0 + sh - 1:a1 - sh], op=ALU.add),
                          reads=(cid, SCRID), writes=(nid,))
                    cur, cid = nxt, nid
                    sh *= 2
                    k += 1
                for (a0, a1), (c0, c1) in zip(segs, parts(ns)):
                    A("dve", lambda e, cur=cur, a0=a0, a1=a1, c0=c0, c1=c1, cc=cc, w=w: e.scalar_tensor_tensor(
                        out=DT(cc)[:, c0:c1], in0=cur[:, a0 + 16:a1], scalar=1.0 / w, in1=PBUF[:, a0 + 16:a1],
                        op0=ALU.mult, op1=ALU.subtract),
                      reads=(cid, "pbuf", SCRID), writes=(("dT", cc),))
                if t == 2:
                    inv = ppS[:, PP_INV + gi * 16: PP_INV + gi * 16 + 16]
                    A("dve", lambda e, cur=cur, inv=inv: e.tensor_tensor(out=PTA[:, 0:16] if cur is not PTA else PTB[:, 0:16],
                                                                        in0=cur[:, 16:32], in1=inv, op=ALU.mult),
                      reads=(cid, "ppS", SCRID), writes=("pta" if cur is not PTA else "ptb",))
                    A("dve", lambda e, cur=cur, cc=cc: e.tensor_tensor(out=DT(cc)[:, 0:16], in0=PTA[:, 0:16] if cur is not PTA else PTB[:, 0:16],
                                                                      in1=PBUF[:, 16:32], op=ALU.subtract),
                      reads=("pta" if cur is not PTA else "ptb", "pbuf", SCRID), writes=(("dT", cc),))
            if full:
                if use_defer:
                    q_block(gi)
                    flush()
                pws = wblock_pool(l, gi)
                for oc in range(2):
                    c = gi * 2 + oc
                    gp = gemm_fm(lambda k, oc=oc, pws=pws: wsl[:, pws, k, oc * 128:(oc + 1) * 128], lambda k, pws=pws: (("w", pws),),
                                 lambda k: DT(k), lambda k: (("dT", k), SCRID), 2, ns)
                    psc = ppS[:, PP_PSC + l * 8 + c: PP_PSC + l * 8 + c + 1]
                    for (pa, c0, c1, pid) in gp:
                        A("act", lambda e, pa=pa, c0=c0, c1=c1, c=c, psc=psc: e.activation(out=AT(c)[:, c0:c1], in_=pa, func=AF.Copy, scale=psc),
                          reads=(pid, "ppS"), writes=(idA(c),))
        if full and not use_defer:
            for bi in range(4):
                q_block(bi)
        kg = ppS[:, PP_KG + l: PP_KG + l + 1]
        for bi in range(4):
            slot = wblock(w_in, l, 0, D, 2048 + bi * 256)
            for cc in range(2):
                h = bi * 2 + cc
                gp = gemm_fm(lambda k, cc=cc, slot=slot: wsl[:, slot, k, cc * 128:(cc + 1) * 128], lambda k, slot=slot: (("w", slot),),
                             lambda k: hT[:, k, :], hids, KC, ns)
                if use_defer:
                    st["held"].add(gp[0][3][1])

                def kfin(gp=gp, h=h):
                    st["held"].discard(gp[0][3][1])
                    if not last:
                        headnorm(gp, kg, [(lambda c0, c1, h=h: KT[:, l, h, half * 512 + c0: half * 512 + c1], ("KT", l, half, h))], 0)
                    else:
                        r = rot("knf", 1)
                        headnorm(gp, kg, [(lambda c0, c1, r=r: KNF(r)[:, c0:c1], ("knf", r))], ns)
                        A("act", lambda e, r=r, h=h: e.activation(out=KT[:, l, h, half * 512: half * 512 + TT], in_=KNF(r)[:, 0:TT], func=AF.Copy),
                          reads=(("knf", r), SCRID), writes=(("KT", l, half, h),))
                        if ns:
                            A("act", lambda e, r=r, h=h: e.activation(out=KTn[:, h, :], in_=KNF(r)[:, TT:TT + NS], func=AF.Copy),
                              reads=(("knf", r), SCRID), writes=(("KTn", h),))
                        if int(WITH_LAST) & 2:
                            kout(l, h, KNF(r), ("knf", r), ns)
                defer(kfin)
        flush()
        for bi in range(4):
            slot = wblock(w_in, l, 0, D, 3072 + bi * 256)
            for sp2 in range(2):
                b = bank()
                for s2 in range(2):
                    s4 = sp2 * 2 + s2
                    for k in range(KC):
                        A("pe", lambda e, k=k, s4=s4, s2=s2, b=b, slot=slot: e.matmul(PSB(b, 256, s2 * 256), hT[:, k, s4 * 128:(s4 + 1) * 128], wsl[:, slot, k, :],
                                                                                       start=(k == 0), stop=(k == KC - 1)),
                          reads=(("h", k), ("w", slot)), writes=(("ps", b),))
                v0 = half * 4 + sp2 * 2
                A("act", lambda e, b=b, v0=v0, bi=bi: e.activation(out=Vc[:, l, v0:v0 + 2, bi * 256:(bi + 1) * 256],
                                                                  in_=PSB(b).rearrange("p (a n) -> p a n", a=2), func=AF.Copy),
                  reads=(("ps", b),), writes=(("Vc", l, half, sp2 * 2), ("Vc", l, half, sp2 * 2 + 1)))
                if last and (int(WITH_LAST) & 4):
                    sl = stg_slot()
                    A("dve", lambda e, b=b, sl=sl: e.tensor_copy(out=stg[:, sl, :], in_=PSB(b)),
                      reads=(("ps", b), ("Vc", l, half, sp2 * 2)), writes=(("stg", sl),))
                    r0 = sp2 * 256
                    A("sp", lambda e, sl=sl, r0=r0, bi=bi: e.dma_start(
                        out=v_o[l, r0:r0 + 256, bi * 256:(bi + 1) * 256].rearrange("(a p) n -> p a n", p=128),
                        in_=stg[:, sl, :].rearrange("p (a n) -> p a n", a=2)),
                      reads=(("stg", sl),), writes=(("OUT", "v", l, sp2, bi),), dkey=("stg", sl))
            if ns:
                b = bank()
                for k in range(KC):
                    A("pe", lambda e, k=k, b=b, slot=slot: e.matmul(ps[0:NS, b * 512: b * 512 + 256], hT[:, k, TT:TT + NS], wsl[:, slot, k, :],
                                                                     start=(k == 0), stop=(k == KC - 1)),
                      reads=(("h", k), ("w", slot)), writes=(("ps", b),))
                A("act", lambda e, b=b, bi=bi: e.activation(out=Vn[:, bi * 256:(bi + 1) * 256], in_=ps[0:NS, b * 512: b * 512 + 256], func=AF.Copy),
                  reads=(("ps", b),), writes=("Vn",))
                sl = stg_slot()
                A("dve", lambda e, b=b, sl=sl: e.tensor_copy(out=stg[0:NS, sl, 0:256], in_=ps[0:NS, b * 512: b * 512 + 256]),
                  reads=(("ps", b), "Vn"), writes=(("stg", sl),))
                A("sp", lambda e, sl=sl, bi=bi: e.dma_start(out=vs_o[l, 512 - NS:512, bi * 256:(bi + 1) * 256], in_=stg[0:NS, sl, 0:256]),
                  reads=(("stg", sl),), writes=(("OUT", "vs", l, bi),), dkey=("stg", sl))
        if not full:
            return

        fence()
        TBST = lambda r: SCRF[:, r * 640:(r + 1) * 640]
        EE = lambda r: SCRF[:, 1280 + r * 640: 1280 + (r + 1) * 640]
        RD = lambda r: SCRF[:, 2560 + r * 128: 2560 + (r + 1) * 128]
        TBSS = lambda r: SCRF[:, 2816 + r * 80: 2816 + (r + 1) * 80]
        ES = lambda r: SCRF[:, 2976 + r * 80: 2976 + (r + 1) * 80]
        EXPB = lambda r: SCRB[:, r * 640:(r + 1) * 640]
        PM = lambda r: SCRB[:, 1280 + r * 640: 1280 + (r + 1) * 640]
        EXPBS = lambda r: SCRB[:, 2560 + r * 80: 2560 + (r + 1) * 80]
        PMS = lambda r: SCRB[:, 2720 + r * 80: 2720 + (r + 1) * 80]

        def keyinfo(m):
            hf = prev if m < 4 else half
            wt = (t - 1) if m < 4 else t
            return hf, m % 4, (0 if wt < 2 else 1)

        iters = [(h, qt) for h in range(H) for qt in range(4)]
        pend = {}

        def emit_S(i):
            h, qt = iters[i]
            if qt == 0:
                rb = rot("tb")
                pend["rb"] = rb
                A("sp", lambda e, rb=rb, h=h: e.dma_start(out=TBST(rb), in_=tb[l, h]), reads=(SCRID,), writes=(("tbst", rb),), dkey=("tbst", rb))
                A("act", lambda e, rb=rb: e.activation(out=EXPB(rb), in_=TBST(rb), func=AF.Exp),
                  reads=(("tbst", rb), SCRID), writes=(("expb", rb),))
                A("dve", lambda e, rb=rb: e.memset(EXPB(rb)[0:64, 64:128], 0.0), reads=(SCRID,), writes=(("expb", rb),))
                A("dve", lambda e, rb=rb: e.memset(EXPB(rb)[64:128, 512:576], 0.0), reads=(SCRID,), writes=(("expb", rb),))
            rb = pend["rb"]
            bp = 2 * (i % 2)
            for j in range(5):
                hf, kk, vi = keyinfo(qt + j)
                A("pe", lambda e, j=j, hf=hf, kk=kk, bp=bp, h=h, qt=qt: e.matmul(
                    ps[:, bp * 512 + j * 128: bp * 512 + (j + 1) * 128], KT[:, l, h, hf * 512 + kk * 128: hf * 512 + (kk + 1) * 128],
                    qT(h)[:, qt * 128:(qt + 1) * 128], start=True, stop=True),
                  reads=(("KT", l, hf, h), idQ(h)), writes=(("ps", bp if j < 4 else bp + 1),))
            re_ = i % 2
            A("act", lambda e, bp=bp, re_=re_: e.activation(out=EE(re_), in_=ps[:, bp * 512: bp * 512 + 640], func=AF.Exp, scale=SCALE),
              reads=(("ps", bp), ("ps", bp + 1), SCRID), writes=(("E", re_),))
            rp = i % 2
            A("pool", lambda e, re_=re_, rp=rp, rb=rb: e.tensor_tensor(out=PM(rp)[:, 0:384], in0=EE(re_)[:, 0:384], in1=EXPB(rb)[:, 0:384], op=ALU.mult),
              reads=(("E", re_), ("expb", rb), SCRID), writes=(("pm", rp, 0),))
            A("dve", lambda e, re_=re_, rp=rp, rb=rb: e.tensor_tensor(out=PM(rp)[:, 384:640], in0=EE(re_)[:, 384:640], in1=EXPB(rb)[:, 384:640], op=ALU.mult),
              reads=(("E", re_), ("expb", rb), SCRID), writes=(("pm", rp, 1),))

        def emit_PV(i):
            h, qt = iters[i]
            rp = i % 2
            bo = 4 if i % 2 == 0 else 6
            bd = 5 if i % 2 == 0 else 7
            for j in range(5):
                hf, kk, vi = keyinfo(qt + j)
                A("pe", lambda e, j=j, hf=hf, kk=kk, bo=bo, rp=rp, h=h: e.matmul(
                    PSB(bo, 128), Vc[:, l, hf * 4 + kk, h * 128:(h + 1) * 128], PM(rp)[:, j * 128:(j + 1) * 128], start=(j == 0), stop=(j == 4)),
                  reads=(("Vc", l, hf, kk), ("pm", rp, 0 if j < 3 else 1)), writes=(("ps", bo),))
            for j in range(5):
                hf, kk, vi = keyinfo(qt + j)
                A("pe", lambda e, j=j, vi=vi, rp=rp, bd=bd: e.matmul(PSB(bd, 128), vonesB[:, vi, :], PM(rp)[:, j * 128:(j + 1) * 128], start=(j == 0), stop=(j == 4)),
                  reads=("vonesB", ("pm", rp, 0 if j < 3 else 1)), writes=(("ps", bd),))
            rd = i % 2
            A("dve", lambda e, rd=rd, bd=bd: e.reciprocal(out=RD(rd), in_=PSB(bd, 128)),
              reads=(("ps", bd), SCRID), writes=(("rd", rd),))
            A("dve", lambda e, rd=rd, bo=bo, h=h, qt=qt: e.tensor_tensor(out=BTt(h)[:, qt * 128:(qt + 1) * 128], in0=PSB(bo, 128), in1=RD(rd), op=ALU.mult),
              reads=(("ps", bo), ("rd", rd), SCRID), writes=(idB(h),))

        emit_S(0)
        for i in range(len(iters)):
            if i + 1 < len(iters):
                emit_S(i + 1)
            emit_PV(i)
        if ns:
            sample_attention(l, prev, TBSS, ES, EXPBS, PMS, RD)

        fence()
        GA = lambda r: SCRF[:, r * NM:(r + 1) * NM]
        GB = lambda r: SCRF[:, (2 + r) * NM:(3 + r) * NM]
        T1 = lambda r: SCRF[:, (4 + r) * NM:(5 + r) * NM]
        for mh in range(2):
            for cp4 in range(4):
                cp = mh * 4 + cp4
                sga = wblock(w_in, l, 0, D, 4096 + cp * 256)
                gas = []
                for cc in range(2):
                    c = cp * 2 + cc
                    gp = gemm_fm(lambda k, cc=cc, sga=sga: wsl[:, sga, k, cc * 128:(cc + 1) * 128], lambda k, sga=sga: (("w", sga),),
                                 lambda k: hT[:, k, :], hids, KC, ns)
                    r = rot("ga")
                    bgc = ppS[:, PP_BG + l * 32 + c: PP_BG + l * 32 + c + 1]
                    for (pa, c0, c1, pid) in gp:
                        A("act", lambda e, pa=pa, c0=c0, c1=c1, r=r, bgc=bgc: e.activation(out=GA(r)[:, c0:c1], in_=pa, func=AF.Sigmoid, bias=bgc),
                          reads=(pid, "ppS", SCRID), writes=(("ga", r),))
                    gas.append(r)
                sgb = wblock(w_in, l, 0, D, 6144 + cp * 256)
                gbs = []
                for cc in range(2):
                    c = cp * 2 + cc
                    gp = gemm_fm(lambda k, cc=cc, sgb=sgb: wsl[:, sgb, k, cc * 128:(cc + 1) * 128], lambda k, sgb=sgb: (("w", sgb),),
                                 lambda k: hT[:, k, :], hids, KC, ns)
                    r = rot("gb")
                    bgc = ppS[:, PP_BG + l * 32 + 16 + c: PP_BG + l * 32 + 16 + c + 1]
                    for (pa, c0, c1, pid) in gp:
                        A("act", lambda e, pa=pa, c0=c0, c1=c1, r=r, bgc=bgc: e.activation(out=GB(r)[:, c0:c1], in_=pa, func=AF.Sigmoid, bias=bgc),
                          reads=(pid, "ppS", SCRID), writes=(("gb", r),))
                    gbs.append(r)
                swa = wblock(w_a, l, 0, 1024, cp * 256)
                t1s = []
                for cc in range(2):
                    gp = gemm_fm(lambda k, cc=cc, swa=swa: wsl[:, swa, k, cc * 128:(cc + 1) * 128], lambda k, swa=swa: (("w", swa),),
                                 lambda k: AT(k), lambda k: (idA(k),), 8, ns)
                    r = rot("t1")
                    for (pa, c0, c1, pid) in gp:
                        A("dve", lambda e, pa=pa, c0=c0, c1=c1, r=r, g=gas[cc]: e.tensor_tensor(out=T1(r)[:, c0:c1], in0=pa, in1=GA(g)[:, c0:c1], op=ALU.mult),
                          reads=(pid, ("ga", gas[cc]), SCRID), writes=(("t1", r),))
                    t1s.append(r)
                swb = wblock(w_b, l, 0, 1024, cp * 256)
                for cc in range(2):
                    c4 = cp4 * 2 + cc
                    gp = gemm_fm(lambda k, cc=cc, swb=swb: wsl[:, swb, k, cc * 128:(cc + 1) * 128], lambda k, swb=swb: (("w", swb),),
                                 lambda k: BTt(k), lambda k: (idB(k),), 8, ns)
                    for (pa, c0, c1, pid) in gp:
                        A("dve", lambda e, pa=pa, c0=c0, c1=c1, g=gbs[cc]: e.tensor_tensor(out=GB(g)[:, c0:c1], in0=pa, in1=GB(g)[:, c0:c1], op=ALU.mult),
                          reads=(pid, ("gb", gbs[cc]), SCRID), writes=(("gb", gbs[cc]),))
                        A("dve", lambda e, c0=c0, c1=c1, r=t1s[cc], c4=c4, g=gbs[cc]: e.tensor_tensor(out=mT(c4)[:, c0:c1], in0=T1(r)[:, c0:c1], in1=GB(g)[:, c0:c1], op=ALU.add),
                          reads=(("gb", gbs[cc]), ("t1", t1s[cc]), SCRID), writes=(idM(c4),))
            for og in range(4):
                so = wblock(w_out, l, mh * 1024, 1024, og * 512, 512)
                for cc in range(4):
                    o = og * 4 + cc
                    gp = gemm_fm(lambda k, cc=cc, so=so: WIDE(so)[:, k, cc * 128:(cc + 1) * 128], lambda k, so=so: (("w", so),),
                                 lambda k: mT(k), lambda k: (idM(k),), 8, ns)
                    for (pa, c0, c1, pid) in gp:
                        A("dve", lambda e, pa=pa, c0=c0, c1=c1, o=o: e.tensor_tensor(out=xT[:, o, c0:c1], in0=pa, in1=xT[:, o, c0:c1], op=ALU.add),
                          reads=(pid, ("x", o)), writes=(("x", o),))

        fence()
        rmsnorm_stream(lambda c: xT[:, c, :], xid, g2, lambda c: hT[:, c, :], hid, KC, ns)

        RT = lambda r: SCRF[:, 2 * NM + r * NM: 2 * NM + (r + 1) * NM]
        for hf2 in range(2):
            for ub in range(16):
                su = wblock(w_up, l, 0, D, hf2 * 4096 + ub * 256)
                for cc in range(2):
                    j = ub * 2 + cc
                    gp = gemm_fm(lambda k, cc=cc, su=su: wsl[:, su, k, cc * 128:(cc + 1) * 128], lambda k, su=su: (("w", su),),
                                 lambda k: hT[:, k, :], hids, KC, ns)
                    r = rot("rt")
                    for (pa, c0, c1, pid) in gp:
                        A("act", lambda e, pa=pa, c0=c0, c1=c1, r=r: e.activation(out=RT(r)[:, c0:c1], in_=pa, func=AF.Relu),
                          reads=(pid, SCRID), writes=(("rt", r),))
                    A("dve", lambda e, r=r, j=j: e.tensor_tensor(out=uT(j)[:, 0:n1], in0=RT(r)[:, 0:n1], in1=RT(r)[:, 0:n1], op=ALU.mult),
                      reads=(("rt", r), SCRID), writes=(idU(j),))
            for og in range(8):
                bks = [bank(), bank()]
                sss = [6, 7] if ns else None
                for r2 in range(2):
                    sdl = wblock(w_dn, l, hf2 * 4096 + r2 * 2048, 2048, og * 256)
                    for cc in range(2):
                        for k in range(16):
                            kk = r2 * 16 + k
                            A("pe", lambda e, k=k, kk=kk, cc=cc, sdl=sdl, b=bks[cc]: e.matmul(
                                PSB(b), wsl[:, sdl, k, cc * 128:(cc + 1) * 128], uT(kk)[:, 0:TT], start=(kk == 0), stop=(kk == 31)),
                              reads=(("w", sdl), idU(kk)), writes=(("ps", bks[cc]),))
                        if ns:
                            for k in range(16):
                                kk = r2 * 16 + k
                                A("pe", lambda e, k=k, kk=kk, cc=cc, sdl=sdl, s_=sss[cc]: e.matmul(
                                    PSS(s_), wsl[:, sdl, k, cc * 128:(cc + 1) * 128], uT(kk)[:, TT:TT + ns], start=(kk == 0), stop=(kk == 31)),
                                  reads=(("w", sdl), idU(kk)), writes=(("ps", sss[cc]),))
                for cc in range(2):
                    o = og * 2 + cc
                    gp = [(PSB(bks[cc]), 0, TT, ("ps", bks[cc]))] + ([(PSS(sss[cc]), TT, TT + ns, ("ps", sss[cc]))] if ns else [])
                    for (pa, c0, c1, pid) in gp:
                        A("dve", lambda e, pa=pa, c0=c0, c1=c1, o=o: e.tensor_tensor(out=xT[:, o, c0:c1], in0=pa, in1=xT[:, o, c0:c1], op=ALU.add),
                          reads=(pid, ("x", o)), writes=(("x", o),))

    def wblock_pool(l, gi):
        src = pool_w[l, gi]
        return ws.next(src, 2, 256)

    def tail_out(l, c, PBUF, ns):
        for (a0, dst, tag) in [(TT, pool_o, "pool")] + ([(544, pools_o, "pools")] if ns else []):
            b = bank()
            A("pe", lambda e, b=b, a0=a0: e.matmul(ps[0:16, b * 512: b * 512 + 128], PBUF[:, a0:a0 + 16], identS[:], start=True, stop=True),
              reads=("pbuf", "identS", SCRID), writes=(("ps", b),))
            sl = stg_slot()
            A("act", lambda e, b=b, sl=sl: e.activation(out=stg[0:16, sl, 0:128], in_=ps[0:16, b * 512: b * 512 + 128], func=AF.Copy),
              reads=(("ps", b),), writes=(("stg", sl),))
            A("sp", lambda e, sl=sl, dst=dst, c=c: e.dma_start(out=dst[l, :, c * 128:(c + 1) * 128], in_=stg[1:16, sl, 0:128]),
              reads=(("stg", sl),), writes=(("OUT", tag, l, c),), dkey=("stg", sl))

    def kout(l, h, knf, knfid, ns):
        b = bank()
        for s4 in range(4):
            A("pe", lambda e, b=b, s4=s4: e.transpose(out=PSB(b, 128, s4 * 128), in_=knf[:, s4 * 128:(s4 + 1) * 128], identity=identS[:]),
              reads=(knfid, "identS", SCRID), writes=(("ps", b),))
        sl = stg_slot()
        A("act", lambda e, b=b, sl=sl: e.activation(out=stg[:, sl, :], in_=PSB(b), func=AF.Copy),
          reads=(("ps", b),), writes=(("stg", sl),))
        A("sp", lambda e, sl=sl: e.dma_start(out=k_o[l, :, h * 128:(h + 1) * 128].rearrange("(a p) n -> p a n", p=128),
                                             in_=stg[:, sl, :].rearrange("p (a n) -> p a n", a=4)),
          reads=(("stg", sl),), writes=(("OUT", "k", l, h),), dkey=("stg", sl))
        if ns:
            b = bank()
            A("pe", lambda e, b=b: e.matmul(ps[0:NS, b * 512: b * 512 + 128], knf[:, TT:TT + NS], identS[:], start=True, stop=True),
              reads=(knfid, "identS", SCRID), writes=(("ps", b),))
            sl = stg_slot()
            A("act", lambda e, b=b, sl=sl: e.activation(out=stg[0:NS, sl, 0:128], in_=ps[0:NS, b * 512: b * 512 + 128], func=AF.Copy),
              reads=(("ps", b),), writes=(("stg", sl),))
            A("sp", lambda e, sl=sl: e.dma_start(out=ks_o[l, 512 - NS:512, h * 128:(h + 1) * 128], in_=stg[0:NS, sl, 0:128]),
              reads=(("stg", sl),), writes=(("OUT", "ks", l, h),), dkey=("stg", sl))

    def sample_attention(l, prev, TBSS, ES, EXPBS, PMS, RD):
        for j in range(4):
            for fh in range(2):
                sl = stg_slot()
                A("sp", lambda e, sl=sl, j=j, fh=fh: e.dma_start(out=stg[:, sl, :], in_=ck[l, j * 128:(j + 1) * 128, fh * 512:(fh + 1) * 512]),
                  writes=(("stg", sl),), dkey=("stg", sl))
                b = bank()
                for hh in range(4):
                    A("pe", lambda e, sl=sl, hh=hh, b=b: e.transpose(out=PSB(b, 128, hh * 128), in_=stg[:, sl, hh * 128:(hh + 1) * 128], identity=identS[:]),
                      reads=(("stg", sl), "identS"), writes=(("ps", b),))
                h0 = fh * 4
                A("act", lambda e, b=b, h0=h0, j=j: e.activation(out=KT[:, l, h0:h0 + 4, prev * 512 + j * 128: prev * 512 + (j + 1) * 128],
                                                                in_=PSB(b).rearrange("p (a n) -> p a n", a=4), func=AF.Copy),
                  reads=(("ps", b),), writes=tuple(("KT", l, prev, h0 + i) for i in range(4)))
                sl = stg_slot()
                A("sp", lambda e, sl=sl, j=j, fh=fh: e.dma_start(out=stg[:, sl, :], in_=cv[l, j * 128:(j + 1) * 128, fh * 512:(fh + 1) * 512]),
                  writes=(("stg", sl),), dkey=("stg", sl))
                A("dve", lambda e, sl=sl, j=j, fh=fh: e.tensor_copy(out=Vc[:, l, prev * 4 + j, fh * 512:(fh + 1) * 512], in_=stg[:, sl, :]),
                  reads=(("stg", sl),), writes=(("Vc", l, prev, j),))
        for h in range(H):
            rb = rot("tbs")
            A("sp", lambda e, rb=rb, h=h: e.dma_start(out=TBSS(rb), in_=tbs[l, h]), reads=(SCRID,), writes=(("tbss", rb),), dkey=("tbss", rb))
            A("act", lambda e, rb=rb: e.activation(out=EXPBS(rb), in_=TBSS(rb), func=AF.Exp),
              reads=(("tbss", rb), SCRID), writes=(("expbs", rb),))
            for j in range(4):
                A("pe", lambda e, j=j, h=h: e.matmul(ps[:, j * NS:(j + 1) * NS], KT[:, l, h, prev * 512 + j * 128: prev * 512 + (j + 1) * 128],
                                                     qT(h)[:, TT:TT + NS], start=True, stop=True),
                  reads=(("KT", l, prev, h), idQ(h)), writes=(("ps", 0),))
            A("pe", lambda e, h=h: e.matmul(ps[0:NS, 4 * NS:5 * NS], KTn[:, h, :], qT(h)[:, TT:TT + NS], start=True, stop=True),
              reads=(("KTn", h), idQ(h)), writes=(("ps", 0),))
            re_ = rot("es")
            A("act", lambda e, re_=re_: e.activation(out=ES(re_)[:, 0:64], in_=ps[:, 0:64], func=AF.Exp, scale=SCALE),
              reads=(("ps", 0), SCRID), writes=(("es", re_),))
            A("act", lambda e, re_=re_: e.activation(out=ES(re_)[0:NS, 64:80], in_=ps[0:NS, 64:80], func=AF.Exp, scale=SCALE),
              reads=(("ps", 0), SCRID), writes=(("es", re_),))
            rp = rot("pms")
            A("dve", lambda e, re_=re_, rp=rp, rb=rb: e.tensor_tensor(out=PMS(rp)[:, 0:64], in0=ES(re_)[:, 0:64], in1=EXPBS(rb)[:, 0:64], op=ALU.mult),
              reads=(("es", re_), ("expbs", rb), SCRID), writes=(("pms", rp),))
            A("dve", lambda e, re_=re_, rp=rp, rb=rb: e.tensor_tensor(out=PMS(rp)[0:NS, 64:80], in0=ES(re_)[0:NS, 64:80], in1=EXPBS(rb)[0:NS, 64:80], op=ALU.mult),
              reads=(("es", re_), ("expbs", rb), SCRID), writes=(("pms", rp),))
            for j in range(4):
                A("pe", lambda e, j=j, rp=rp, h=h: e.matmul(PSB(4, NS), Vc[:, l, prev * 4 + j, h * 128:(h + 1) * 128], PMS(rp)[:, j * NS:(j + 1) * NS],
                                                            start=(j == 0), stop=False),
                  reads=(("Vc", l, prev, j), ("pms", rp)), writes=(("ps", 4),))
            A("pe", lambda e, rp=rp, h=h: e.matmul(PSB(4, NS), Vn[:, h * 128:(h + 1) * 128], PMS(rp)[0:NS, 64:80], start=False, stop=True),
              reads=("Vn", ("pms", rp)), writes=(("ps", 4),))
            for j in range(4):
                A("pe", lambda e, j=j, rp=rp: e.matmul(PSB(5, NS), vonesB[:, 1, :], PMS(rp)[:, j * NS:(j + 1) * NS], start=(j == 0), stop=False),
                  reads=("vonesB", ("pms", rp)), writes=(("ps", 5),))
            A("pe", lambda e, rp=rp: e.matmul(PSB(5, NS), vonesB[0:NS, 1, :], PMS(rp)[0:NS, 64:80], start=False, stop=True),
              reads=("vonesB", ("pms", rp)), writes=(("ps", 5),))
            rd = rot("rd")
            A("dve", lambda e, rd=rd: e.reciprocal(out=RD(rd)[:, 0:NS], in_=PSB(5, NS)),
              reads=(("ps", 5), SCRID), writes=(("rd", rd),))
            A("dve", lambda e, rd=rd, h=h: e.tensor_tensor(out=BTt(h)[:, TT:TT + NS], in0=PSB(4, NS), in1=RD(rd)[:, 0:NS], op=ALU.mult),
              reads=(("ps", 4), ("rd", rd), SCRID), writes=(idB(h),))

    def sample_setup():
        for l in range(L):
            for fh in range(2):
                sl = stg_slot()
                A("sp", lambda e, sl=sl, l=l, fh=fh: e.dma_start(out=stg[0:15, sl, :], in_=sph[l, :, fh * 512:(fh + 1) * 512]),
                  writes=(("stg", sl),), dkey=("stg", sl))
                b = bank()
                for cc in range(4):
                    A("pe", lambda e, sl=sl, cc=cc, b=b: e.transpose(out=PSB(b, 15, cc * 16 + 1), in_=stg[0:15, sl, cc * 128:(cc + 1) * 128], identity=identS[0:15, 0:15]),
                      reads=(("stg", sl), "identS"), writes=(("ps", b),))
                A("dve", lambda e, b=b, l=l, fh=fh: e.tensor_copy(out=phs[:, l, fh * 4:fh * 4 + 4, 1:16],
                                                                 in_=PSB(b, 64).rearrange("p (a n) -> p a n", a=4)[:, :, 1:16]),
                  reads=(("ps", b),), writes=(("phs", l),))
            A("sp", lambda e, l=l: e.dma_start(out=ks_o[l, 0:512 - NS, :], in_=ck[l, NS:512, :]), writes=(("OUT", "ksc", l),), dkey=("ksc", l))
            A("sp", lambda e, l=l: e.dma_start(out=vs_o[l, 0:512 - NS, :], in_=cv[l, NS:512, :]), writes=(("OUT", "vsc", l),), dkey=("vsc", l))

    def body(with_sample=WITH_SAMPLE):
        stop = STOP
        if stop >= 1:
            sample_setup()
        for t in range(4):
            wsm = int(with_sample) if with_sample is not True else 7
            ns0 = NS if (t == 3 and (wsm & 1)) else 0
            ns1 = NS if (t == 3 and (wsm & 2)) else 0
            nsy = NS if (t == 3 and (wsm & 4)) else 0
            last = (t == 3) and bool(WITH_LAST)
            if stop >= 2 + 3 * t:
                load_x(t, NS if (t == 3 and wsm) else 0)
            if stop >= 3 + 3 * t:
                run_pass(0, t, full=(t >= 1), ns=ns0, last=last)
            if t >= 1 and stop >= 4 + 3 * t:
                run_pass(1, t, full=(t >= 2), ns=ns1, last=last)
                if t >= 2:
                    store_y(t, nsy)
        outs = set()
        for o in prog.ops:
            for w_ in o.writes:
                if isinstance(w_, tuple) and w_ and w_[0] == "OUT":
                    outs.add(w_)
        A("sp", None, reads=tuple(sorted(outs, key=str)), writes=())

    ws.record = True
    body()
    plan = ws.plan
    prog.ops = []
    st.update({"bank": 0, "ss": 0, "ss6": 0, "stg": 0, "rot": {}, "held": set()})
    ws.record = False
    ws.pos = 0
    ws.emitted = 0
    A("sp", lambda e: e.dma_start(out=ppS[:], in_=pp), writes=("ppS",), dkey="ppS")
    A("sp", lambda e: e.dma_start(out=identS[:], in_=ident), writes=("identS",), dkey="identS")
    A("sp", lambda e: e.dma_start(out=stg[:, 0, 0:256].rearrange("p (a n) -> p a n", a=2), in_=vones), writes=(("stg", 0),), dkey=("stg", 0))
    A("dve", lambda e: e.tensor_copy(out=vonesB[:], in_=stg[:, 0, 0:256].rearrange("p (a n) -> p a n", a=2)), reads=(("stg", 0),), writes=("vonesB",))
    A("dve", lambda e: e.memset(phist[:], 0.0), writes=tuple(("phist", l) for l in range(L)))
    A("dve", lambda e: e.memset(phs[:], 0.0), writes=tuple(("phs", l) for l in range(L)))
    A("dve", lambda e: e.memset(dummy[:], 0.0), writes=(SCRID,))
    body()
    assert ws.pos == len(plan), (ws.pos, len(plan))

    prog.analyze()
    sems = {}
    for en in ("pe", "act", "dve", "pool", "sp"):
        sems[("e", en)] = es.enter_context(nc.semaphore("s_" + en))
    for i, dk in enumerate(prog.dkeys):
        sems[("d", dk)] = es.enter_context(nc.semaphore("d%d" % i))
    block = es.enter_context(nc.Block())
    per = {en: [o for o in prog.ops if o.eng == en] for en in ("pe", "act", "dve", "pool", "sp")}

    def emit(en, e):
        for o in per[en]:
            for (key, val) in o.waits:
                e.wait_ge(sems[key], val)
            if o.fn is None:
                continue
            ins = o.fn(e)
            if o.sig:
                if o.dkey is not None:
                    ins.then_inc(sems[("d", o.dkey)], 16)
                else:
                    ins.then_inc(sems[("e", en)], 1)

    @block.tensor
    def _(e):
        emit("pe", e)

    @block.scalar
    def _(e):
        emit("act", e)

    @block.vector
    def _(e):
        emit("dve", e)

    @block.gpsimd
    def _(e):
        emit("pool", e)

    @block.sync
    def _(e):
        emit("sp", e)

    es.close()
    return nc, len(prog.ops)


def _host_layout(inp):
    f = lambda a: np.ascontiguousarray(np.asarray(a, dtype=np.float32))
    x_prompt = f(inp["x_prompt"]); x_sample = f(inp["x_sample"])
    state_pool = f(inp["state_pool"]); cache_k = f(inp["cache_k"]); cache_v = f(inp["cache_v"])
    rel_bias = f(inp["rel_bias"])
    def tile_w(w, rb, cb):
        l_, r_, c_ = w.shape
        t = w.reshape(l_, r_ // rb, rb // 128, 128, c_ // cb, cb)
        return np.ascontiguousarray(t.transpose(0, 1, 4, 3, 2, 5))

    shared = {
        "w_in": tile_w(f(inp["w_in"]), 2048, 256),
        "pool_w": np.ascontiguousarray(f(inp["pool_w"]).reshape(L, 4, 2, 128, 256).transpose(0, 1, 3, 2, 4)),
        "w_a": tile_w(f(inp["w_branch_a"]), 1024, 256), "w_b": tile_w(f(inp["w_branch_b"]), 1024, 256),
        "w_out": tile_w(f(inp["w_out"]), 1024, 512), "w_up": tile_w(f(inp["w_up"]), 2048, 256),
        "w_dn": tile_w(f(inp["w_down"]), 2048, 256),
        "ident": np.eye(128, dtype=np.float32),
    }
    kj = np.arange(640)[:, None]
    qi = np.arange(128)[None, :]
    idx = np.clip(qi - kj + 512, -128, 128) + 128
    tbf = rel_bias[:, :, idx]
    tbf = tbf.reshape(L, H, 5, 128, 128).transpose(0, 1, 3, 2, 4).reshape(L, H, 128, 640)
    shared["tb"] = np.ascontiguousarray(tbf)
    kpos = np.concatenate([512 + np.arange(512), 1024 + np.arange(16), np.full(112, 1024)])
    qpos = 1024 + np.arange(16)
    idxs = np.clip(qpos[None, :] - kpos[:, None], -128, 128) + 128
    tbsf = rel_bias[:, :, idxs].reshape(L, H, 5, 128, 16).transpose(0, 1, 3, 2, 4).reshape(L, H, 128, 80)
    shared["tbs"] = np.ascontiguousarray(tbsf)
    ppb = np.zeros((128, NPP), np.float32)
    for l in range(L):
        ppb[:, PP_G1 + l * 16: PP_G1 + (l + 1) * 16] = f(inp["norm1_g"])[l].reshape(16, 128).T
        ppb[:, PP_G2 + l * 16: PP_G2 + (l + 1) * 16] = f(inp["norm2_g"])[l].reshape(16, 128).T
        ppb[:, PP_BG + l * 32: PP_BG + (l + 1) * 32] = f(inp["b_gate"])[l].reshape(32, 128).T
        ppb[:, PP_PSC + l * 8: PP_PSC + (l + 1) * 8] = f(inp["pool_scale"])[l].reshape(8, 128).T
        ppb[:, PP_QG + l] = f(inp["q_norm_g"])[l]
        ppb[:, PP_KG + l] = f(inp["k_norm_g"])[l]
    in_maps = []
    for c in range(NCORES):
        b, q = divmod(c, 4)
        xw = np.zeros((WIN, D), np.float32)
        s0 = q * 1024 - 1024
        lo = max(s0, 0)
        xw[lo - s0:] = x_prompt[b, lo:(q + 1) * 1024]
        ppc = ppb.copy()
        for gi, w in enumerate(POOL_W):
            pos = q * 1024 + np.arange(16)
            ppc[:, PP_INV + gi * 16: PP_INV + (gi + 1) * 16] = (1.0 / np.minimum(pos + 1, w)).astype(np.float32)[None, :]
        vo = np.ones((128, 2, 128), np.float32)
        if q == 0:
            vo[:, 0, :] = 1e-30
        m = dict(shared)
        m.update({
            "xw": xw, "xs": x_sample[c], "sph": state_pool[:, c], "ck": cache_k[:, c].reshape(L, 512, 1024),
            "cv": cache_v[:, c].reshape(L, 512, 1024), "pp": ppc, "vones": vo,
        })
        in_maps.append(m)
    return in_maps


_CACHE = {}
_STOP = [99]


def kernel(**inputs):
    if "nc" not in _CACHE:
        _CACHE["nc"] = build_program(*_STOP)
    nc, _ = _CACHE["nc"]
    in_maps = _host_layout(inputs)
    res = run_bass_kernel_spmd(nc, in_maps, core_ids=list(range(NCORES)))
    r = res.results
    y_prompt = np.zeros((2, 4096, D), np.float32)
    for c in range(NCORES):
        b, q = divmod(c, 4)
        y_prompt[b, q * 1024:(q + 1) * 1024] = r[c]["y_o"]
    y_sample = np.stack([r[c]["ys_o"] for c in range(NCORES)])
    last = [3, 7]
    pool_prompt = np.stack([r[c]["pool_o"] for c in last], axis=1)
    k_prompt = np.stack([r[c]["k_o"] for c in last], axis=1).reshape(L, 2, 512, H, 128)
    v_prompt = np.stack([r[c]["v_o"] for c in last], axis=1).reshape(L, 2, 512, H, 128)
    pool_sample = np.stack([r[c]["pools_o"] for c in range(NCORES)], axis=1)
    k_sample = np.stack([r[c]["ks_o"] for c in range(NCORES)], axis=1).reshape(L, NCORES, 512, H, 128)
    v_sample = np.stack([r[c]["vs_o"] for c in range(NCORES)], axis=1).reshape(L, NCORES, 512, H, 128)
    f = lambda a: np.ascontiguousarray(a, dtype=np.float32)
    return (f(y_prompt), f(y_sample), f(pool_prompt), f(k_prompt), f(v_prompt), f(pool_sample), f(k_sample), f(v_sample))
```
